# Optimizing a Trainium2 kernel written in Bass

```python
import math
import jax, jax.numpy as jnp
from jax import lax
import numpy as np

D_MODEL = 1024
BATCH = 4
SEQ = 8192
DEPTH = 2

GRID_W = 64
CTX_LEN = 256
N_MIXERS = 2
N_HEADS = 16
HEAD_DIM = D_MODEL // N_HEADS
NA_KH = 8
NA_KW = 16
SWA_KV_HEADS = 4
SWA_GROUP = N_HEADS // SWA_KV_HEADS
SWA_WINDOW = 128
SWA_BLOCK = 128
D_FF = 4 * D_MODEL
ROPE_BASE = 10000.0
NORM_EPS = 1e-6
NEG_INF = -1e30
N_NA_LAYERS = (DEPTH + 1) // 2
N_SWA_LAYERS = DEPTH // 2

kernel_name = "hybrid_natten_swa_prefix_dit"


def rms_norm(x, g):
    xf = x.astype(jnp.float32)
    y = xf * lax.rsqrt(jnp.mean(xf * xf, axis=-1, keepdims=True) + NORM_EPS)
    return (y * g.astype(jnp.float32)).astype(x.dtype)


def modulate(h, shift, scale):
    return h * (1 + scale) + shift


def squared_relu_mlp(h, w1, w2):
    return jnp.square(jax.nn.relu(h @ w1)) @ w2


def axial_rope_tables(L):
    t = jnp.arange(L, dtype=jnp.int32)
    row = (t // GRID_W).astype(jnp.float32)
    col = (t % GRID_W).astype(jnp.float32)
    n_freq = HEAD_DIM // 4
    inv = ROPE_BASE ** (-jnp.arange(n_freq, dtype=jnp.float32) / n_freq)
    ang = jnp.stack([row[:, None] * inv, col[:, None] * inv], axis=1)
    return jnp.cos(ang), jnp.sin(ang)


def apply_axial_rope(x, cos, sin):
    B, L, h, dh = x.shape
    xr = x.astype(jnp.float32).reshape(B, L, h, 2, 2, dh // 4)
    x1, x2 = xr[..., 0, :], xr[..., 1, :]
    cb = cos[None, :, None]
    sb = sin[None, :, None]
    out = jnp.stack([x1 * cb - x2 * sb, x1 * sb + x2 * cb], axis=-2)
    return out.reshape(B, L, h, dh).astype(x.dtype)


def context_attention(q, k, v, sink=None):
    B, C, KV, G, dh = q.shape
    s = jnp.einsum('bqkgd,bskd->bkgqs', q, k, preferred_element_type=jnp.float32) * (dh ** -0.5)
    if sink is not None:
        s_sink = jnp.broadcast_to(sink.reshape(1, KV, G, 1, 1).astype(jnp.float32), (B, KV, G, C, 1))
        s = jnp.concatenate([s, s_sink], axis=-1)
    p = jax.nn.softmax(s, axis=-1)[..., :C].astype(v.dtype)
    o = jnp.einsum('bkgqs,bskd->bqkgd', p, v)
    return o.reshape(B, C, KV * G * dh)


def neighborhood_attention(q, k, v, kc, vc, rpb):
    B, L, H, dh = q.shape
    rows = L // GRID_W
    kh = min(NA_KH, rows)
    kw = min(NA_KW, GRID_W)
    n_loc = kh * kw
    scale = dh ** -0.5
    qg = q.reshape(B, rows, GRID_W, H, dh)
    kg = k.reshape(B, rows, GRID_W, H, dh)
    vg = v.reshape(B, rows, GRID_W, H, dh)
    cols = jnp.arange(GRID_W)
    col_start = jnp.clip(cols - kw // 2, 0, GRID_W - kw)
    col_idx = col_start[:, None] + jnp.arange(kw)[None, :]
    col_off = col_idx - cols[:, None] + (NA_KW - 1)

    def one_row(r):
        r0 = jnp.clip(r - kh // 2, 0, rows - kh)
        k_rows = lax.dynamic_slice_in_dim(kg, r0, kh, axis=1)
        v_rows = lax.dynamic_slice_in_dim(vg, r0, kh, axis=1)
        k_win = k_rows[:, :, col_idx]
        v_win = v_rows[:, :, col_idx]
        q_row = lax.dynamic_index_in_dim(qg, r, axis=1, keepdims=False)
        s_loc = jnp.einsum('bqhd,bkqwhd->bhqkw', q_row, k_win,
                           preferred_element_type=jnp.float32) * scale
        row_off = r0 + jnp.arange(kh) - r + (NA_KH - 1)
        bias = rpb[:, row_off[:, None, None], col_off[None, :, :]]
        s_loc = s_loc + jnp.transpose(bias, (0, 2, 1, 3)).astype(jnp.float32)[None]
        s_ctx = jnp.einsum('bqhd,bchd->bhqc', q_row, kc,
                           preferred_element_type=jnp.float32) * scale
        s = jnp.concatenate([s_loc.reshape(B, H, GRID_W, n_loc), s_ctx], axis=-1)
        p = jax.nn.softmax(s, axis=-1).astype(v.dtype)
        p_loc = p[..., :n_loc].reshape(B, H, GRID_W, kh, kw)
        p_ctx = p[..., n_loc:]
        return (jnp.einsum('bhqkw,bkqwhd->bqhd', p_loc, v_win)
                + jnp.einsum('bhqc,bchd->bqhd', p_ctx, vc))

    out = lax.map(one_row, jnp.arange(rows))
    return jnp.transpose(out, (1, 0, 2, 3, 4)).reshape(B, L, H * dh)


def sliding_window_attention(q, k, v, kc, vc, sink):
    B, L, KV, G, dh = q.shape
    C = kc.shape[1]
    T = SWA_BLOCK
    nb = L // T
    scale = dh ** -0.5
    qb = q.reshape(B, nb, T, KV, G, dh)
    pad = jnp.zeros((B, T, KV, dh), k.dtype)
    kp = jnp.concatenate([pad, k, pad], axis=1)
    vp = jnp.concatenate([pad, v, pad], axis=1)
    rel = (jnp.arange(3 * T)[None, :] - T) - jnp.arange(T)[:, None]
    s_sink = jnp.broadcast_to(sink.reshape(1, KV, G, 1, 1).astype(jnp.float32), (B, KV, G, T, 1))

    def one_block(j):
        q_blk = lax.dynamic_index_in_dim(qb, j, axis=1, keepdims=False)
        k_blk = lax.dynamic_slice_in_dim(kp, j * T, 3 * T, axis=1)
        v_blk = lax.dynamic_slice_in_dim(vp, j * T, 3 * T, axis=1)
        kpos = (j - 1) * T + jnp.arange(3 * T)
        valid = (jnp.abs(rel) <= SWA_WINDOW) & ((kpos >= 0) & (kpos < L))[None, :]
        s_loc = jnp.einsum('bqkgd,bskd->bkgqs', q_blk, k_blk,
                           preferred_element_type=jnp.float32) * scale
        s_loc = jnp.where(valid, s_loc, NEG_INF)
        s_ctx = jnp.einsum('bqkgd,bckd->bkgqc', q_blk, kc,
                           preferred_element_type=jnp.float32) * scale
        s = jnp.concatenate([s_loc, s_ctx, s_sink], axis=-1)
        p = jax.nn.softmax(s, axis=-1).astype(v.dtype)
        return (jnp.einsum('bkgqs,bskd->bqkgd', p[..., :3 * T], v_blk)
                + jnp.einsum('bkgqc,bckd->bqkgd', p[..., 3 * T:3 * T + C], vc))

    out = lax.map(one_block, jnp.arange(nb))
    return jnp.transpose(out, (1, 0, 2, 3, 4, 5)).reshape(B, L, KV * G * dh)


def setup_inputs(seed: int = 0) -> dict:
    key = jax.random.key(seed)
    ks = jax.random.split(key, 24)
    f = jnp.float32
    d = D_MODEL
    kv_w = SWA_KV_HEADS * HEAD_DIM

    def nrm(k, shape, scale):
        return jax.random.normal(k, shape, f) * scale

    return {
        "x": nrm(ks[0], (BATCH, SEQ, d), 1.0),
        "c": nrm(ks[1], (BATCH, d), 1.0),
        "ctx": nrm(ks[2], (BATCH, CTX_LEN, d), 1.0),
        "c_ctx": nrm(ks[3], (d,), 1.0),
        "ada_w": nrm(ks[4], (DEPTH, d, 6 * d), 0.5 * d ** -0.5),
        "ada_b": nrm(ks[5], (DEPTH, 6 * d), 0.02),
        "g_mix": 1.0 + nrm(ks[6], (DEPTH, d), 0.02),
        "g_mlp": 1.0 + nrm(ks[7], (DEPTH, d), 0.02),
        "mlp_w1": nrm(ks[8], (DEPTH, d, D_FF), d ** -0.5),
        "mlp_w2": nrm(ks[9], (DEPTH, D_FF, d), D_FF ** -0.5),
        "na_wqkv": nrm(ks[10], (N_NA_LAYERS, d, 3 * d), d ** -0.5),
        "na_q_gain": 1.0 + nrm(ks[11], (N_NA_LAYERS, HEAD_DIM), 0.02),
        "na_k_gain": 1.0 + nrm(ks[12], (N_NA_LAYERS, HEAD_DIM), 0.02),
        "na_rpb": nrm(ks[13], (N_NA_LAYERS, N_HEADS, 2 * NA_KH - 1, 2 * NA_KW - 1), 0.1),
        "na_wo": nrm(ks[14], (N_NA_LAYERS, d, d), d ** -0.5),
        "swa_wqkv": nrm(ks[15], (N_SWA_LAYERS, d, d + 2 * kv_w), d ** -0.5),
        "swa_q_gain": 1.0 + nrm(ks[16], (N_SWA_LAYERS, HEAD_DIM), 0.02),
        "swa_k_gain": 1.0 + nrm(ks[17], (N_SWA_LAYERS, HEAD_DIM), 0.02),
        "swa_sink": nrm(ks[18], (N_SWA_LAYERS, N_HEADS), 0.5),
        "swa_wo": nrm(ks[19], (N_SWA_LAYERS, d, d), d ** -0.5),
    }


def reference(x, c, ctx, c_ctx, ada_w, ada_b, g_mix, g_mlp, mlp_w1, mlp_w2,
              na_wqkv, na_q_gain, na_k_gain, na_rpb, na_wo,
              swa_wqkv, swa_q_gain, swa_k_gain, swa_sink, swa_wo):
    B, L, D = x.shape
    C = ctx.shape[1]
    H, dh, KV, G = N_HEADS, HEAD_DIM, SWA_KV_HEADS, SWA_GROUP
    cos, sin = axial_rope_tables(L)
    silu_c = jax.nn.silu(c)
    silu_cc = jax.nn.silu(c_ctx)
    h_lat = x
    h_ctx = ctx
    for i in range(DEPTH):
        last = i == DEPTH - 1
        j = i // N_MIXERS
        mod_lat = (silu_c @ ada_w[i] + ada_b[i])[:, None, :]
        mod_ctx = (silu_cc @ ada_w[i] + ada_b[i])[None, None, :]
        sh1, sc1, ga1, sh2, sc2, ga2 = jnp.split(mod_lat, 6, axis=-1)
        csh1, csc1, cga1, csh2, csc2, cga2 = jnp.split(mod_ctx, 6, axis=-1)
        u_lat = modulate(rms_norm(h_lat, g_mix[i]), sh1, sc1)
        u_ctx = modulate(rms_norm(h_ctx, g_mix[i]), csh1, csc1)

        if i % N_MIXERS == 0:
            w = na_wqkv[j]
            ql, kl, vl = jnp.split((u_lat @ w).reshape(B, L, 3, H, dh), 3, axis=2)
            qc, kc, vc = jnp.split((u_ctx @ w).reshape(B, C, 3, H, dh), 3, axis=2)
            ql, kl, vl = ql[:, :, 0], kl[:, :, 0], vl[:, :, 0]
            qc, kc, vc = qc[:, :, 0], kc[:, :, 0], vc[:, :, 0]
            ql, qc = rms_norm(ql, na_q_gain[j]), rms_norm(qc, na_q_gain[j])
            kl, kc = rms_norm(kl, na_k_gain[j]), rms_norm(kc, na_k_gain[j])
            y_lat = neighborhood_attention(ql, kl, vl, kc, vc, na_rpb[j]) @ na_wo[j]
            if not last:
                y_ctx = context_attention(qc[:, :, :, None, :], kc, vc) @ na_wo[j]
        else:
            w = swa_wqkv[j]
            pl = u_lat @ w
            pc = u_ctx @ w
            kvw = KV * dh
            ql = pl[..., :D].reshape(B, L, H, dh)
            kl = pl[..., D:D + kvw].reshape(B, L, KV, dh)
            vl = pl[..., D + kvw:].reshape(B, L, KV, dh)
            qc = pc[..., :D].reshape(B, C, H, dh)
            kc = pc[..., D:D + kvw].reshape(B, C, KV, dh)
            vc = pc[..., D + kvw:].reshape(B, C, KV, dh)
            ql, qc = rms_norm(ql, swa_q_gain[j]), rms_norm(qc, swa_q_gain[j])
            kl, kc = rms_norm(kl, swa_k_gain[j]), rms_norm(kc, swa_k_gain[j])
            ql = apply_axial_rope(ql, cos, sin)
            kl = apply_axial_rope(kl, cos, sin)
            y_lat = sliding_window_attention(ql.reshape(B, L, KV, G, dh), kl, vl,
                                             kc, vc, swa_sink[j]) @ swa_wo[j]
            if not last:
                y_ctx = context_attention(qc.reshape(B, C, KV, G, dh), kc, vc,
                                          sink=swa_sink[j]) @ swa_wo[j]

        h_lat = h_lat + ga1 * y_lat
        h_lat = h_lat + ga2 * squared_relu_mlp(
            modulate(rms_norm(h_lat, g_mlp[i]), sh2, sc2), mlp_w1[i], mlp_w2[i])
        if not last:
            h_ctx = h_ctx + cga1 * y_ctx
            h_ctx = h_ctx + cga2 * squared_relu_mlp(
                modulate(rms_norm(h_ctx, g_mlp[i]), csh2, csc2), mlp_w1[i], mlp_w2[i])
    return h_lat
```

```python
import numpy as np
from contextlib import ExitStack
import concourse.bass as bass
import concourse.mybir as mybir
from concourse.bass_utils import run_bass_kernel_spmd

F32 = mybir.dt.float32
BF16 = mybir.dt.bfloat16
AF = mybir.ActivationFunctionType
ALU = mybir.AluOpType
AX = mybir.AxisListType

D = 1024
H = 16
DH = 64
KV = 4
DFF = 4096
GRID_W = 64
ROWS = 128
CTX = 256
EPS = 1e-6
NEG = -200.0
NLAT = 38
NT = 40
CT = (38, 39)
SPECIAL = (3, 4, 33, 34)
ARENA_WORDS = 53000


class Prog:
    ND = 48

    def __init__(self, nc, es):
        self.nc = nc
        self.engs = ["sync", "act", "pool", "dve", "pe"]
        self.q = {e: [] for e in self.engs}
        self.cnt = {e: 0 for e in self.engs}
        self.sem = {e: es.enter_context(nc.semaphore("s_" + e)) for e in ["act", "pool", "dve", "pe"]}
        self.dsem = {q: [es.enter_context(nc.semaphore("d_%s%d" % (q, i))) for i in range(self.ND)]
                     for q in ["sync", "pool"]}
        self.dcnt = {q: [0] * self.ND for q in ["sync", "pool"]}
        self.drr = {q: 0 for q in ["sync", "pool"]}
        self.lastw = {}
        self.readers = {}
        self.pending_barrier = {e: None for e in self.engs}
        self.nops = 0

    def _deps(self, eng, reads, writes):
        deps = {}

        def add(t):
            if t is None:
                return
            k, v = t
            if deps.get(k, 0) < v:
                deps[k] = v
        for k in reads:
            add(self.lastw.get(k))
        for k in writes:
            add(self.lastw.get(k))
            for t in self.readers.get(k, {}).items():
                add(t)
        pb = self.pending_barrier[eng]
        if pb is not None:
            for t in pb.items():
                add(t)
            self.pending_barrier[eng] = None
        own = ("c", eng)
        if own in deps:
            if eng == "pe":
                del deps[own]
            elif deps[own] > self.cnt[eng]:
                deps[own] = self.cnt[eng]
                if deps[own] == 0:
                    del deps[own]
        return deps

    def _note(self, ticket, reads, writes):
        k0, v0 = ticket
        for k in reads:
            r = self.readers.setdefault(k, {})
            if r.get(k0, 0) < v0:
                r[k0] = v0
        for k in writes:
            self.lastw[k] = ticket
            self.readers[k] = {}

    def op(self, eng, fn, reads=(), writes=(), inc=True):
        deps = self._deps(eng, reads, writes)
        if inc:
            self.cnt[eng] += 1
            ticket = (("c", eng), self.cnt[eng])
        else:
            ticket = (("c", eng), self.cnt[eng] + 1)
        self.q[eng].append((fn, deps, ("c", inc)))
        self._note(ticket, reads, writes)
        self.nops += 1
        return ticket

    def dma(self, queue, fn, reads=(), writes=()):
        deps = self._deps(queue, reads, writes)
        i = self.drr[queue]
        self.drr[queue] = (i + 1) % self.ND
        if self.dcnt[queue][i]:
            k = ("d", queue, i)
            if deps.get(k, 0) < self.dcnt[queue][i]:
                deps[k] = self.dcnt[queue][i]
        self.dcnt[queue][i] += 16
        ticket = (("d", queue, i), self.dcnt[queue][i])
        self.q[queue].append((fn, deps, ("d", i)))
        self._note(ticket, reads, writes)
        self.nops += 1
        return ticket

    def snapshot(self):
        snap = {}
        for e in ["act", "pool", "dve", "pe"]:
            if self.cnt[e]:
                snap[("c", e)] = self.cnt[e]
        for q in ["sync", "pool"]:
            for i in range(self.ND):
                if self.dcnt[q][i]:
                    snap[("d", q, i)] = self.dcnt[q][i]
        return snap

    def barrier(self):
        snap = self.snapshot()
        for e in self.engs:
            self.pending_barrier[e] = dict(snap)
        self.lastw = {}
        self.readers = {}

    def finish(self):
        snap = self.snapshot()
        self.q["sync"].append((None, snap, None))

    def _semh(self, k):
        if k[0] == "c":
            return self.sem[k[1]]
        return self.dsem[k[1]][k[2]]

    def emit(self, eng, e):
        known = {}
        for fn, deps, kind in self.q[eng]:
            for k, v in deps.items():
                if known.get(k, 0) >= v:
                    continue
                e.wait_ge(self._semh(k), v)
                known[k] = v
            if fn is None:
                continue
            ins = fn(e)
            if kind[0] == "c":
                if kind[1]:
                    ins.then_inc(self.sem[eng], 1)
            else:
                ins.then_inc(self.dsem[eng][kind[1]], 16)


class Arena:
    def __init__(self, ar, nwords):
        self.ar = ar
        self.n = nwords
        self.off = 0
        self.base = 0

    def mark(self):
        self.base = self.off

    def reset(self):
        self.off = self.base

    def f32(self, n):
        a = self.ar[:, self.off:self.off + n]
        self.off += n
        assert self.off <= self.n, ("arena overflow", self.off, self.n)
        return a

    def bf16(self, n):
        w = (n + 1) // 2
        a = self.ar[:, self.off:self.off + w].bitcast(BF16)
        self.off += w
        assert self.off <= self.n, ("arena overflow", self.off, self.n)
        return a


def build_nc(debug=False, upto=99):
    nc = bass.Bass("TRN2", target_bir_lowering=False)
    skind = "ExternalOutput" if debug else "Internal"

    def din(name, shape, dt=F32):
        return nc.dram_tensor(name, list(shape), dt, kind="ExternalInput").ap()

    def dscr(name, shape, dt):
        return nc.dram_tensor(name, list(shape), dt, kind=skind).ap()

    xext = din("xext", [NT, 128, D])
    cT = din("cT", [128, 16])
    ada_w = din("ada_w", [2, D, 6 * D])
    ada_b = din("ada_b", [2, 6 * D])
    g_mix = din("g_mix", [2, D])
    g_mlp = din("g_mlp", [2, D])
    w1d = din("mlp_w1", [2, D, DFF])
    w2d = din("mlp_w2", [2, DFF, D])
    wqkv0 = din("wqkv0", [D, 3 * D])
    wqkv1 = din("wqkv1", [D, D + 2 * KV * DH])
    wo0 = din("wo0", [D, D])
    wo1 = din("wo1", [D, D])
    gains = din("gains", [4, DH])
    sink = din("sink", [1, H])
    bint = din("bint", [128, H * 640])
    bspec = din("bspec", [4, 128, H * 896])
    mswa = din("mswa", [3, 128, 384])
    csd = din("cs", [NT, 128, 64])
    identd = din("ident", [128, 128])
    outd = nc.dram_tensor("out", [32, 128, D], F32, kind="ExternalOutput").ap()

    MODS = dscr("MODS", [24, 128, D], F32)
    QTd = dscr("QTd", [NT, 128, D], BF16)
    KTd = dscr("KTd", [NT, 128, D], BF16)
    VAd = dscr("VAd", [NT, 128, H * 65], BF16)
    H1 = dscr("H1", [NT, 128, D], F32)
    H2 = dscr("H2", [NT, 128, D], F32)
    H3 = dscr("H3", [NT, 128, D], F32)

    es = ExitStack()
    with es:
        AR = es.enter_context(nc.sbuf_tensor("AR", [128, ARENA_WORDS], F32))
        PS = [es.enter_context(nc.psum_tensor("PS%d" % i, [128, 512], F32)) for i in range(7)]
        PTR = es.enter_context(nc.psum_tensor("PTR", [128, 1024], BF16))
        P = Prog(nc, es)
        A = Arena(AR, ARENA_WORDS)

        ID = A.bf16(128)
        P.dma("pool", lambda e: e.dma_start(out=ID, in_=identd[:, :]), writes=["ID"])
        A.mark()

        def transposes(src, srckey, n, dst_fn=None):
            for i in range(n):
                P.op("pe", lambda e, i=i: e.transpose(out=PTR[:, i * 128:(i + 1) * 128],
                                                      in_=src[:, i * 128:(i + 1) * 128], identity=ID),
                     reads=[srckey, "ID"], writes=["PTR"], inc=(i == n - 1))

        def norm_mod(X, xkey, JK, SS, RS, T1, U, GS, SH, tag, ukey="U"):
            P.op("act", lambda e: e.activation(out=JK, in_=X, func=AF.Square, accum_out=SS),
                 reads=[xkey], writes=["JK", "SS" + tag])
            P.op("act", lambda e: e.activation(out=RS, in_=SS, func=AF.Sqrt, bias=EPS, scale=1.0 / D),
                 reads=["SS" + tag], writes=["RS" + tag])
            P.op("dve", lambda e: e.reciprocal(out=RS, in_=RS), reads=["RS" + tag], writes=["RS" + tag])
            P.op("dve", lambda e: e.scalar_tensor_tensor(out=T1, in0=X, scalar=RS, in1=GS,
                                                         op0=ALU.mult, op1=ALU.mult),
                 reads=[xkey, "RS" + tag, "GS"], writes=["T1"])
            P.op("pool", lambda e: e.tensor_tensor(out=U, in0=T1, in1=SH, op=ALU.add),
                 reads=["T1", "SH"], writes=[ukey])

        def load_w_cast(dst3, src2, nk, keyp, ncols, split=1):
            sv = src2.rearrange("(k p) n -> p k n", p=128)
            cw = ncols // split
            for s in range(split):
                P.dma("pool", lambda e, s=s: e.dma_start(out=dst3[:, :, s * cw:(s + 1) * cw],
                                                         in_=sv[:, :, s * cw:(s + 1) * cw]),
                      writes=["%s_%d" % (keyp, s)])
            return lambda k, col: "%s_%d" % (keyp, col // cw)

        if upto >= 0:
            A.reset()
            CTs = A.f32(16)
            SIL = A.f32(16)
            SB = A.bf16(16 * 128).rearrange("p (r k m) -> p r k m", r=2, k=8)
            WA = [A.bf16(8 * 512).rearrange("p (k n) -> p k n", k=8) for _ in range(4)]
            BB = [A.f32(512) for _ in range(4)]
            GG = [A.f32(512) for _ in range(4)]
            TMP = [A.f32(512) for _ in range(2)]
            OUT = [A.f32(512) for _ in range(4)]
            P.dma("sync", lambda e: e.dma_start(out=CTs, in_=cT[:, :]), writes=["CT"])
            P.op("act", lambda e: e.activation(out=SIL, in_=CTs, func=AF.Silu), reads=["CT"], writes=["SIL"])
            P.op("dve", lambda e: e.tensor_copy(out=SB.rearrange("p r k m -> p (r k) m"),
                                                in_=SIL.unsqueeze(2).broadcast_to([128, 16, 128])),
                 reads=["SIL"], writes=["SB"])
            it = 0
            for l in range(2):
                awv = ada_w[l].rearrange("(k p) n -> p k n", p=128)
                for g in range(12):
                    j = g // 2
                    half = g % 2
                    kind = j % 3
                    b = it % 4
                    P.dma("pool", lambda e, b=b, g=g, awv=awv: e.dma_start(
                        out=WA[b][:, :, :], in_=awv[:, :, g * 512:(g + 1) * 512]), writes=["WA%d" % b])
                    P.dma("sync", lambda e, b=b, l=l, g=g: e.dma_start(
                        out=BB[b], in_=ada_b[l:l + 1, g * 512:(g + 1) * 512].partition_broadcast(128)),
                        writes=["BB%d" % b])
                    if kind == 1:
                        gsrc = g_mix if j == 1 else g_mlp
                        P.dma("sync", lambda e, b=b, l=l, half=half, gsrc=gsrc: e.dma_start(
                            out=GG[b], in_=gsrc[l:l + 1, half * 512:(half + 1) * 512].partition_broadcast(128)),
                            writes=["GG%d" % b])
                    for ty in range(2):
                        pb = (it % 2) * 2 + ty
                        for k in range(8):
                            P.op("pe", lambda e, pb=pb, ty=ty, k=k, b=b: e.matmul(
                                PS[pb][:, :], lhsT=SB[:, ty, k, :], rhs=WA[b][:, k, :], start=(k == 0), stop=(k == 7)),
                                reads=["SB", "WA%d" % b], writes=["PS%d" % pb], inc=(k == 7))
                        ob = (it % 2) * 2 + ty
                        okey = "OUT%d" % ob
                        if kind == 1:
                            P.op("dve", lambda e, pb=pb, b=b: e.tensor_tensor(out=TMP[b % 2], in0=PS[pb][:, :], in1=BB[b], op=ALU.add),
                                 reads=["PS%d" % pb, "BB%d" % b], writes=["TMP%d" % (b % 2)])
                            P.op("dve", lambda e, ob=ob, b=b: e.scalar_tensor_tensor(
                                out=OUT[ob], in0=TMP[b % 2], scalar=1.0, in1=GG[b], op0=ALU.add, op1=ALU.mult),
                                reads=["TMP%d" % (b % 2), "GG%d" % b], writes=[okey])
                        else:
                            P.op("dve", lambda e, pb=pb, b=b, ob=ob: e.tensor_tensor(out=OUT[ob], in0=PS[pb][:, :], in1=BB[b], op=ALU.add),
                                 reads=["PS%d" % pb, "BB%d" % b], writes=[okey])
                        mi = (l * 2 + ty) * 6 + j
                        P.dma("sync", lambda e, ob=ob, mi=mi, half=half: e.dma_start(
                            out=MODS[mi][:, half * 512:(half + 1) * 512], in_=OUT[ob]),
                            reads=[okey], writes=["MODS%d_%d" % (mi, half)])
                    it += 1
            P.barrier()

        def load_mod(dst, mi, key):
            P.dma("sync", lambda e: e.dma_start(out=dst, in_=MODS[mi][:, :]),
                  reads=["MODS%d_0" % mi, "MODS%d_1" % mi], writes=[key])

        def phase1(l):
            A.reset()
            nqkv = 3 * D if l == 0 else D + 2 * KV * DH
            ng = nqkv // 512
            hk = H if l == 0 else KV
            W = A.bf16(8 * nqkv).rearrange("p (k n) -> p k n", k=8)
            GS = A.f32(D)
            SH = A.f32(D)
            GQ = A.f32(DH)
            GK = A.f32(DH)
            GQc = A.f32(1)
            GKc = A.f32(1)
            X = [A.f32(D) for _ in range(3)]
            T1 = A.f32(D)
            JK = A.bf16(D)
            SS = [A.f32(1) for _ in range(2)]
            RS = [A.f32(1) for _ in range(2)]
            U = A.bf16(D)
            UT = [A.bf16(D) for _ in range(2)]
            SQ = [A.f32(512) for _ in range(4)]
            SSQ = A.f32(32)
            RQ = A.f32(32)
            QN = A.f32(D)
            KN = A.f32(D)
            QH = [A.bf16(D) for _ in range(2)]
            KH = [A.bf16(D) for _ in range(2)]
            QT = [A.bf16(D) for _ in range(2)]
            KT = [A.bf16(D) for _ in range(2)]
            VA = [A.bf16(H * 65) for _ in range(2)]
            if l == 1:
                RA = [A.f32(512) for _ in range(2)]
                RB = [A.f32(512) for _ in range(2)]
                NR = 34
                CSall = A.f32(NR * 64).rearrange("p (t c) -> p t c", c=64)
                TBqA = A.f32(NR * 128).rearrange("p (t c) -> p t c", c=128)
                TBkA = A.f32(NR * 128).rearrange("p (t c) -> p t c", c=128)
            PTRQ = PS[6][:, :].bitcast(BF16)
            wsrc = wqkv0 if l == 0 else wqkv1
            wkey = load_w_cast(W, wsrc, 8, "W", nqkv, split=ng)
            P.dma("sync", lambda e: e.dma_start(out=GQ, in_=gains[2 * l:2 * l + 1, :].partition_broadcast(128)), writes=["GQ"])
            P.dma("sync", lambda e: e.dma_start(out=GK, in_=gains[2 * l + 1:2 * l + 2, :].partition_broadcast(128)), writes=["GK"])
            P.op("dve", lambda e: e.tensor_scalar_mul(out=GQ, in0=GQ, scalar1=0.125), reads=["GQ"], writes=["GQ"])
            if l == 0:
                for hp in range(2):
                    P.dma("sync", lambda e, hp=hp: e.dma_start(out=GQc[hp * 64:(hp + 1) * 64, :],
                                                              in_=gains[0:1, :].rearrange("o d -> d o")), writes=["GQc%d" % hp])
                    P.dma("sync", lambda e, hp=hp: e.dma_start(out=GKc[hp * 64:(hp + 1) * 64, :],
                                                              in_=gains[1:2, :].rearrange("o d -> d o")), writes=["GKc%d" % hp])
                P.op("dve", lambda e: e.tensor_scalar_mul(out=GQc, in0=GQc, scalar1=0.125),
                     reads=["GQc0", "GQc1"], writes=["GQc"])
            for b in range(2):
                P.op("pool", lambda e, b=b: e.memset(VA[b], 1.0), writes=["VA%d" % b])
            if l == 1:
                P.dma("sync", lambda e: e.dma_start(out=CSall, in_=csd[2:2 + NR].rearrange("t p c -> p t c")), writes=["CSall"])
                cosv = CSall[:, :, 0:32].rearrange("p t (a f) -> p t a f", a=2)
                sinv = CSall[:, :, 32:64].rearrange("p t (a f) -> p t a f", a=2)
                for (TBA, G_, gk_, tk_, eng_) in ((TBqA, GQ, "GQ", "TBqA", "pool"), (TBkA, GK, "GK", "TBkA", "dve")):
                    g4 = G_.rearrange("p (a s f) -> p a s f", a=2, s=2)
                    tb4 = TBA.rearrange("p t (i a f) -> p t i a f", i=4, a=2)
                    for i_, (cs_, sidx) in enumerate(((cosv, 0), (sinv, 1), (sinv, 0), (cosv, 1))):
                        P.op(eng_, lambda e, i_=i_, cs_=cs_, sidx=sidx, g4=g4, tb4=tb4: e.tensor_tensor(
                            out=tb4[:, :, i_, :, :], in0=cs_,
                            in1=g4[:, :, sidx, :].unsqueeze(1).broadcast_to([128, NR, 2, 16]), op=ALU.mult),
                            reads=["CSall", gk_], writes=[tk_])
            tiles = list(range(NLAT)) if l == 0 else list(range(2, 36))
            tiles = tiles + list(CT)
            n = len(tiles)
            src = xext if l == 0 else H2
            st = {"ty": None}
            gk_b = GK.unsqueeze(1).broadcast_to([128, hk, DH])

            def info(it):
                t = tiles[it]
                ty = 1 if t in CT else 0
                return t, ty, it % 2, it % 3, (l == 1 and ty == 0), not (l == 1 and ty == 1)

            def pbank(it, g):
                return g if l == 0 else (it % 2) * 3 + g

            def stageA(it):
                t, ty, b2, xs, rope, need_q = info(it)
                if ty != st["ty"]:
                    load_mod(SH, (l * 2 + ty) * 6 + 0, "SH")
                    load_mod(GS, (l * 2 + ty) * 6 + 1, "GS")
                    st["ty"] = ty
                xk = "X%d" % xs
                skey = ("H2_%d" % t) if l == 1 else None
                P.dma("sync", lambda e: e.dma_start(out=X[xs], in_=src[t][:, :]),
                      reads=([skey] if skey else []), writes=[xk])
                norm_mod(X[xs], xk, JK, SS[b2], RS[b2], T1, U, GS, SH, str(b2))

            def stageAtr(it):
                t, ty, b2, xs, rope, need_q = info(it)
                transposes(U, "U", 8)
                P.op("act", lambda e: e.copy(out=UT[b2], in_=PTR[:, :]), reads=["PTR"], writes=["UT%d" % b2])

            def stageB(it, mid=None):
                t, ty, b2, xs, rope, need_q = info(it)
                for g in range(ng):
                    if mid is not None and g == (ng + 1) // 2:
                        mid()
                    bk = pbank(it, g)
                    for k in range(8):
                        P.op("pe", lambda e, g=g, k=k, bk=bk: e.matmul(PS[bk][:, :], lhsT=UT[b2][:, k * 128:(k + 1) * 128],
                                                                      rhs=W[:, k, g * 512:(g + 1) * 512],
                                                                      start=(k == 0), stop=(k == 7)),
                             reads=["UT%d" % b2, wkey(k, g * 512)], writes=["PS%d" % bk], inc=(k == 7))

            def rope_ops(srcT, skey, nh, outs, okey, TB, tkey):
                s5 = srcT.rearrange("p (h a s f) -> p h a s f", a=2, s=2, f=16)
                x1 = s5[:, :, :, 0, :]
                x2 = s5[:, :, :, 1, :]
                tb3 = TB.rearrange("p (i a f) -> p i a f", i=4, a=2)
                tbb = [tb3[:, i_, :, :].unsqueeze(1).broadcast_to([128, nh, 2, 16]) for i_ in range(4)]
                ra = [RA[i][:, 0:nh * 32].rearrange("p (h a f) -> p h a f", a=2, f=16) for i in range(2)]
                rb = [RB[i][:, 0:nh * 32].rearrange("p (h a f) -> p h a f", a=2, f=16) for i in range(2)]
                P.op("dve", lambda e: e.tensor_tensor(out=ra[0], in0=x1, in1=tbb[0], op=ALU.mult), reads=[skey, tkey], writes=["RA0"])
                P.op("pool", lambda e: e.tensor_tensor(out=rb[0], in0=x2, in1=tbb[1], op=ALU.mult), reads=[skey, tkey], writes=["RB0"])
                P.op("dve", lambda e: e.tensor_tensor(out=ra[1], in0=x1, in1=tbb[2], op=ALU.mult), reads=[skey, tkey], writes=["RA1"])
                P.op("pool", lambda e: e.tensor_tensor(out=rb[1], in0=x2, in1=tbb[3], op=ALU.mult), reads=[skey, tkey], writes=["RB1"])
                for o in outs:
                    P.op("dve", lambda e, o=o: e.tensor_tensor(out=o[:, :, :, 0, :], in0=ra[0], in1=rb[0], op=ALU.subtract),
                         reads=["RA0", "RB0"], writes=[okey])
                    P.op("pool", lambda e, o=o: e.tensor_tensor(out=o[:, :, :, 1, :], in0=ra[1], in1=rb[1], op=ALU.add),
                         reads=["RA1", "RB1"], writes=[okey])

            def stageC(it):
                t, ty, b2, xs, rope, need_q = info(it)
                qk = "QH%d" % b2
                kk = "KH%d" % b2
                segs = []
                if need_q:
                    segs += [(pbank(it, 0), 512, 0, 8, "q"), (pbank(it, 1), 512, 8, 8, "q")]
                if l == 0:
                    segs += [(2, 512, 16, 8, "k"), (3, 512, 24, 8, "k")]
                else:
                    segs += [(pbank(it, 2), 256, 16, 4, "k")]
                for i, (bk, ncol, s0, nh, kind) in enumerate(segs):
                    P.op("act", lambda e, bk=bk, ncol=ncol, i=i: e.activation(
                        out=SQ[i][:, 0:ncol], in_=PS[bk][:, 0:ncol], func=AF.Square),
                        reads=["PS%d" % bk], writes=["SQ%d" % i])
                for i, (bk, ncol, s0, nh, kind) in enumerate(segs):
                    P.op("dve", lambda e, ncol=ncol, s0=s0, nh=nh, i=i: e.tensor_reduce(
                        out=SSQ[:, s0:s0 + nh], in_=SQ[i][:, 0:ncol].rearrange("p (h d) -> p h d", d=DH),
                        axis=AX.X, op=ALU.add), reads=["SQ%d" % i], writes=["SSQ"])
                if l == 0:
                    for i2, bk in enumerate((4, 5)):
                        P.op("act", lambda e, i2=i2, bk=bk: e.copy(
                            out=VA[b2].rearrange("p (h d) -> p h d", d=65)[:, i2 * 8:(i2 + 1) * 8, 0:64],
                            in_=PS[bk][:, :].rearrange("p (h d) -> p h d", d=DH)),
                            reads=["PS%d" % bk], writes=["VA%d" % b2])
                    vcols = H * 65
                else:
                    bkv = pbank(it, 2)
                    P.op("act", lambda e: e.copy(
                        out=VA[b2][:, 0:KV * 65].rearrange("p (h d) -> p h d", d=65)[:, :, 0:64],
                        in_=PS[bkv][:, 256:512].rearrange("p (h d) -> p h d", d=DH)),
                        reads=["PS%d" % bkv], writes=["VA%d" % b2])
                    vcols = KV * 65
                P.dma("sync", lambda e: e.dma_start(out=VAd[t][:, 0:vcols], in_=VA[b2][:, 0:vcols]),
                      reads=["VA%d" % b2], writes=["VAd%d" % t])
                c_lo = 0 if need_q else 16
                c_hi = 32 if l == 0 else 20
                P.op("act", lambda e: e.activation(out=RQ[:, c_lo:c_hi], in_=SSQ[:, c_lo:c_hi], func=AF.Sqrt, bias=EPS, scale=1.0 / DH),
                     reads=["SSQ"], writes=["RQ"])
                P.op("dve", lambda e: e.reciprocal(out=RQ[:, c_lo:c_hi], in_=RQ[:, c_lo:c_hi]), reads=["RQ"], writes=["RQ"])
                for (bk, ncol, s0, nh, kind) in segs:
                    d0 = (s0 % 16) * DH
                    if l == 0:
                        dst, dkey = (QH[b2], qk) if kind == "q" else (KH[b2], kk)
                    else:
                        dst, dkey = (QN, "QN") if kind == "q" else (KN, "KN")
                    P.op("dve", lambda e, bk=bk, ncol=ncol, s0=s0, nh=nh, d0=d0, dst=dst: e.tensor_tensor(
                        out=dst[:, d0:d0 + ncol].rearrange("p (h d) -> p h d", d=DH),
                        in0=PS[bk][:, 0:ncol].rearrange("p (h d) -> p h d", d=DH),
                        in1=RQ[:, s0:s0 + nh].unsqueeze(2).broadcast_to([128, nh, DH]), op=ALU.mult),
                        reads=["PS%d" % bk, "RQ"], writes=[dkey])
                if l == 1:
                    khd = KH[b2][:, 0:KV * 128].rearrange("p (h r d) -> p h r d", h=KV, r=2)
                    if rope:
                        q5 = QH[b2].rearrange("p (h a s f) -> p h a s f", a=2, s=2, f=16)
                        rope_ops(QN, "QN", H, [q5], qk, TBqA[:, t - 2, :], "TBqA")
                        k5 = [khd[:, :, r, :].rearrange("p h (a s f) -> p h a s f", a=2, s=2) for r in range(2)]
                        rope_ops(KN[:, 0:KV * DH], "KN", KV, k5, kk, TBkA[:, t - 2, :], "TBkA")
                    else:
                        for r in range(2):
                            P.op("pool", lambda e, r=r: e.tensor_tensor(
                                out=khd[:, :, r, :], in0=KN[:, 0:KV * DH].rearrange("p (h d) -> p h d", d=DH),
                                in1=gk_b, op=ALU.mult), reads=["KN", "GK"], writes=[kk])

            def trq(srcT, srckey, nn):
                for i in range(nn):
                    P.op("pe", lambda e, i=i: e.transpose(out=PTRQ[:, i * 128:(i + 1) * 128],
                                                          in_=srcT[:, i * 128:(i + 1) * 128], identity=ID),
                         reads=[srckey, "ID"], writes=["PS6"], inc=(i == nn - 1))

            def evac(dstT, dkey, ncols, gcol, gkey):
                if l == 0:
                    P.op("act", lambda e: e.activation(out=dstT[:, 0:ncols], in_=PTRQ[:, 0:ncols], func=AF.Copy, scale=gcol[:, 0:1]),
                         reads=["PS6"] + gkey, writes=[dkey])
                else:
                    P.op("act", lambda e: e.copy(out=dstT[:, 0:ncols], in_=PTRQ[:, 0:ncols]), reads=["PS6"], writes=[dkey])

            def stageDq(it):
                t, ty, b2, xs, rope, need_q = info(it)
                if need_q:
                    trq(QH[b2], "QH%d" % b2, 8)
                    evac(QT[b2], "QT%d" % b2, D, GQc, ["GQc"])
                    P.dma("sync", lambda e: e.dma_start(out=QTd[t][:, :], in_=QT[b2]),
                          reads=["QT%d" % b2], writes=["QTd%d" % t])

            def stageDk(it):
                t, ty, b2, xs, rope, need_q = info(it)
                nkt = 8 if l == 0 else KV
                trq(KH[b2], "KH%d" % b2, nkt)
                evac(KT[b2], "KT%d" % b2, nkt * 128, GKc, ["GKc0", "GKc1"])
                P.dma("sync", lambda e: e.dma_start(out=KTd[t][:, 0:nkt * 128], in_=KT[b2][:, 0:nkt * 128]),
                      reads=["KT%d" % b2], writes=["KTd%d" % t])

            stageA(0)
            stageAtr(0)
            for s in range(n + 1):
                if s + 1 < n:
                    stageA(s + 1)
                if s >= 1:
                    stageDq(s - 1)
                if s < n:
                    stageB(s, mid=((lambda s=s: stageDk(s - 1)) if s >= 1 else None))
                    stageC(s)
                elif s >= 1:
                    stageDk(s - 1)
                if s + 1 < n:
                    stageAtr(s + 1)
            P.barrier()

        def phase2(l):
            A.reset()
            hk = H if l == 0 else KV
            kcols = (8 if l == 0 else KV) * 128
            vcols = hk * 65
            NS = 8
            WO = A.bf16(8 * D).rearrange("p (k n) -> p k n", k=8)
            GA = A.f32(D)
            KR = [A.bf16(kcols) for _ in range(NS)]
            VR = [A.bf16(vcols) for _ in range(NS)]
            KC = [A.bf16(kcols) for _ in range(2)]
            VC = [A.bf16(vcols) for _ in range(2)]
            Q = [A.bf16(D) for _ in range(2)]
            HR = [A.f32(D) for _ in range(2)]
            T = [A.f32(896) for _ in range(2)]
            PT = [A.bf16(1152) for _ in range(5)]
            DEN = A.f32(H)
            RD = A.f32(H)
            AO = A.bf16(D)
            AOT = A.bf16(D)
            T2 = A.f32(D)
            HN = [A.f32(D) for _ in range(2)]
            if l == 0:
                BI = A.f32(H * 640)
                BS = [A.f32(896) for _ in range(2)]
                P.dma("sync", lambda e: e.dma_start(out=BI[:, 0:H * 320], in_=bint[:, 0:H * 320]), writes=["BIa"])
                P.dma("sync", lambda e: e.dma_start(out=BI[:, H * 320:], in_=bint[:, H * 320:]), writes=["BIb"])
            else:
                MS = A.f32(3 * 384)
                ESK = A.f32(H)
                P.dma("sync", lambda e: e.dma_start(out=MS.rearrange("p (v c) -> p v c", v=3),
                                                    in_=mswa.rearrange("v p c -> p v c")), writes=["MS"])
                P.dma("sync", lambda e: e.dma_start(out=ESK, in_=sink[0:1, :].partition_broadcast(128)), writes=["ESK"])
                P.op("act", lambda e: e.activation(out=ESK, in_=ESK, func=AF.Exp), reads=["ESK"], writes=["ESK"])
            wokey = load_w_cast(WO, wo0 if l == 0 else wo1, 8, "WO", D, split=2)
            for i, t in enumerate(CT):
                P.dma("sync", lambda e, i=i, t=t: e.dma_start(out=KC[i], in_=KTd[t][:, 0:kcols]),
                      reads=["KTd%d" % t], writes=["KC%d" % i])
                P.dma("sync", lambda e, i=i, t=t: e.dma_start(out=VC[i], in_=VAd[t][:, 0:vcols]),
                      reads=["VAd%d" % t], writes=["VC%d" % i])
            if l == 0:
                blocks = list(range(2, 36)) + list(CT)
                lo_t, hi_t = 0, NLAT - 1
            else:
                blocks = list(range(3, 35))
                lo_t, hi_t = 2, 35
            src = xext if l == 0 else H2
            dst = H1 if l == 0 else H3
            state = {"next": lo_t, "ty": None}

            def ensure_loaded(tmax):
                while state["next"] <= min(tmax, hi_t):
                    tt = state["next"]
                    s = tt % NS
                    P.dma("sync", lambda e, s=s, tt=tt: e.dma_start(out=KR[s], in_=KTd[tt][:, 0:kcols]),
                          reads=["KTd%d" % tt], writes=["KR%d" % s])
                    P.dma("sync", lambda e, s=s, tt=tt: e.dma_start(out=VR[s], in_=VAd[tt][:, 0:vcols]),
                          reads=["VAd%d" % tt], writes=["VR%d" % s])
                    state["next"] += 1

            binfo = {}

            def prologue(ib):
                t = blocks[ib]
                isctx = t in CT
                ty = 1 if isctx else 0
                spec = (l == 0 and t in SPECIAL)
                if isctx:
                    loc = []
                elif l == 0:
                    r = 3 if spec else 2
                    loc = list(range(t - r, t + r + 1))
                else:
                    loc = [t - 1, t, t + 1]
                if loc:
                    ensure_loaded(loc[-1] + 1)
                qb = ib % 2
                P.dma("sync", lambda e: e.dma_start(out=Q[qb], in_=QTd[t][:, :]),
                      reads=["QTd%d" % t], writes=["Q%d" % qb])
                skey = ("H2_%d" % t) if l == 1 else None
                P.dma("sync", lambda e: e.dma_start(out=HR[qb], in_=src[t][:, :]),
                      reads=([skey] if skey else []), writes=["HR%d" % qb])
                chunks = [(KR[tt % NS], VR[tt % NS], "KR%d" % (tt % NS), "VR%d" % (tt % NS)) for tt in loc]
                chunks += [(KC[i], VC[i], "KC%d" % i, "VC%d" % i) for i in range(2)]
                binfo[ib] = dict(t=t, ty=ty, spec=spec, nloc=len(loc), qb=qb, chunks=chunks)

            items = [(ib, h) for ib in range(len(blocks)) for h in range(H)]
            hinfo = {}

            def scores(s):
                ib, h = items[s]
                if h == 0:
                    prologue(ib)
                bi_ = binfo[ib]
                t, spec, nloc, qb, chunks = bi_["t"], bi_["spec"], bi_["nloc"], bi_["qb"], bi_["chunks"]
                nch = len(chunks)
                nlc = nloc * 128
                nbk = (nch + 3) // 4
                sb = 0 if nbk == 3 else (s % 2) * 2
                tb = s % 2
                pb = s % 5
                hinfo[s] = pb
                base = (h % 2) * 64
                qp = h // 2
                ks = (h // 2) if l == 0 else (h // 4)
                for c, (kc_, vc_, kk, vk) in enumerate(chunks):
                    bank = sb + c // 4
                    col = (c % 4) * 128
                    P.op("pe", lambda e, bank=bank, col=col, kc_=kc_: e.matmul(
                        PS[bank][:, col:col + 128], lhsT=kc_[base:base + 64, ks * 128:(ks + 1) * 128],
                        rhs=Q[qb][base:base + 64, qp * 128:(qp + 1) * 128], start=True, stop=True),
                        reads=[kk, "Q%d" % qb], writes=["PS%d" % bank], inc=(c == nch - 1 or c % 4 == 3))
                if nloc:
                    if l == 0:
                        if spec:
                            si = SPECIAL.index(t)
                            bsb = h % 2
                            P.dma("sync", lambda e: e.dma_start(
                                out=BS[bsb], in_=bspec[si][:, h * 896:(h + 1) * 896]), writes=["BS%d" % bsb])
                            bias = BS[bsb]
                            bkeys = ["BS%d" % bsb]
                        else:
                            bias = BI[:, h * 640:(h + 1) * 640]
                            bkeys = ["BIa", "BIb"]
                    else:
                        v = 1 if t == 3 else (2 if t == 34 else 0)
                        bias = MS[:, v * 384:(v + 1) * 384]
                        bkeys = ["MS"]
                    for bi in range(nbk):
                        a0 = bi * 512
                        a1 = min(nlc, a0 + 512)
                        if a1 <= a0:
                            continue
                        P.op("dve", lambda e, bi=bi, a0=a0, a1=a1: e.tensor_tensor(
                            out=T[tb][:, a0:a1], in0=PS[sb + bi][:, 0:a1 - a0], in1=bias[:, a0:a1], op=ALU.add),
                            reads=["PS%d" % (sb + bi)] + bkeys, writes=["T%d" % tb])
                    P.op("act", lambda e: e.activation(out=PT[pb][:, 0:nlc], in_=T[tb][:, 0:nlc], func=AF.Exp),
                         reads=["T%d" % tb], writes=["PT%d" % pb])
                for bi in range(nbk):
                    a0 = max(nlc, bi * 512)
                    a1 = min(nch * 128, (bi + 1) * 512)
                    if a1 <= a0:
                        continue
                    P.op("act", lambda e, bi=bi, a0=a0, a1=a1: e.activation(
                        out=PT[pb][:, a0:a1], in_=PS[sb + bi][:, a0 - bi * 512:a1 - bi * 512], func=AF.Exp),
                        reads=["PS%d" % (sb + bi)], writes=["PT%d" % pb])

            def pv(s):
                ib, h = items[s]
                chunks = binfo[ib]["chunks"]
                nch = len(chunks)
                pb = hinfo[s]
                vh = h if l == 0 else h // 4
                ob = 4 + h // 7
                oc = (h % 7) * 65
                for c, (kc_, vc_, kk, vk) in enumerate(chunks):
                    P.op("pe", lambda e, c=c, vc_=vc_: e.matmul(
                        PS[ob][:, oc:oc + 65], lhsT=PT[pb][:, c * 128:(c + 1) * 128],
                        rhs=vc_[:, vh * 65:(vh + 1) * 65], start=(c == 0), stop=(c == nch - 1)),
                        reads=["PT%d" % pb, vk], writes=["PS%d" % ob], inc=(c == nch - 1))

            groups = [(4, 0, 7), (5, 7, 7), (6, 14, 2)]

            def tailA(ib):
                for (bk, h0, nh) in groups:
                    P.op("dve", lambda e, bk=bk, h0=h0, nh=nh: e.tensor_copy(
                        out=DEN[:, h0:h0 + nh].unsqueeze(2),
                        in_=PS[bk][:, 0:nh * 65].rearrange("p (h d) -> p h d", d=65)[:, :, 64:65]),
                        reads=["PS%d" % bk], writes=["DEN"])
                if l == 1:
                    P.op("dve", lambda e: e.tensor_tensor(out=DEN, in0=DEN, in1=ESK, op=ALU.add),
                         reads=["DEN", "ESK"], writes=["DEN"])
                P.op("dve", lambda e: e.reciprocal(out=RD, in_=DEN), reads=["DEN"], writes=["RD"])
                for (bk, h0, nh) in groups:
                    P.op("dve", lambda e, bk=bk, h0=h0, nh=nh: e.tensor_tensor(
                        out=AO[:, h0 * DH:(h0 + nh) * DH].rearrange("p (h d) -> p h d", d=DH),
                        in0=PS[bk][:, 0:nh * 65].rearrange("p (h d) -> p h d", d=65)[:, :, 0:64],
                        in1=RD[:, h0:h0 + nh].unsqueeze(2).broadcast_to([128, nh, DH]), op=ALU.mult),
                        reads=["PS%d" % bk, "RD"], writes=["AO"])

            def tailB(ib):
                transposes(AO, "AO", 8)
                P.op("act", lambda e: e.copy(out=AOT, in_=PTR[:, :]), reads=["PTR"], writes=["AOT"])

            def tailC(ib):
                bi_ = binfo[ib]
                t, ty, qb = bi_["t"], bi_["ty"], bi_["qb"]
                if ty != state["ty"]:
                    load_mod(GA, (l * 2 + ty) * 6 + 2, "GA")
                    state["ty"] = ty
                for g in range(2):
                    for k in range(8):
                        P.op("pe", lambda e, g=g, k=k: e.matmul(PS[g][:, :], lhsT=AOT[:, k * 128:(k + 1) * 128],
                                                                rhs=WO[:, k, g * 512:(g + 1) * 512], start=(k == 0), stop=(k == 7)),
                             reads=["AOT", wokey(k, g * 512)], writes=["PS%d" % g], inc=(k == 7))
                for g in range(2):
                    P.op("dve", lambda e, g=g: e.tensor_tensor(out=T2[:, g * 512:(g + 1) * 512], in0=PS[g][:, :],
                                                               in1=GA[:, g * 512:(g + 1) * 512], op=ALU.mult),
                         reads=["PS%d" % g, "GA"], writes=["T2"])
                P.op("pool", lambda e: e.tensor_tensor(out=HN[qb], in0=T2, in1=HR[qb], op=ALU.add),
                     reads=["T2", "HR%d" % qb], writes=["HN%d" % qb])
                dk = ("H1_%d" if l == 0 else "H3_%d") % t
                P.dma("sync", lambda e: e.dma_start(out=dst[t][:, :], in_=HN[qb]),
                      reads=["HN%d" % qb], writes=[dk])

            ni = len(items)
            for s in range(ni + 5):
                if s < ni:
                    scores(s)
                if 0 <= s - 3 < ni:
                    pv(s - 3)
                    if items[s - 3][1] == H - 1:
                        tailA(items[s - 3][0])
                if 0 <= s - 4 < ni and items[s - 4][1] == H - 1:
                    tailB(items[s - 4][0])
                if 0 <= s - 5 < ni and items[s - 5][1] == H - 1:
                    tailC(items[s - 5][0])
            P.barrier()

        def phase3(l):
            A.reset()
            W1 = A.bf16(8 * DFF).rearrange("p (k n) -> p k n", k=8)
            W2 = A.bf16(32 * D).rearrange("p (j n) -> p j n", j=32)
            GS = A.f32(D)
            SH = A.f32(D)
            GA = A.f32(D)
            X = [A.f32(D) for _ in range(6)]
            T1 = A.f32(D)
            JK = A.bf16(D)
            SS = [A.f32(1) for _ in range(2)]
            RS = [A.f32(1) for _ in range(2)]
            U = [A.bf16(D) for _ in range(2)]
            UT2 = [A.bf16(8 * 256).rearrange("p (k n) -> p k n", k=8) for _ in range(2)]
            HT = A.bf16(32 * 256)
            R = [A.f32(512) for _ in range(2)]
            T2 = A.f32(D)
            w1key = load_w_cast(W1, w1d[l], 8, "W1_", DFF, split=4)
            w2v = w2d[l].rearrange("(j p) n -> p j n", p=128)
            for jq in range(4):
                P.dma("pool", lambda e, jq=jq: e.dma_start(out=W2[:, jq * 8:(jq + 1) * 8, :], in_=w2v[:, jq * 8:(jq + 1) * 8, :]),
                      writes=["W2_%d" % jq])
            if l == 0:
                tiles = list(range(2, 36)) + list(CT)
                src, dst = H1, H2
                sk, dk = "H1_%d", "H2_%d"
            else:
                tiles = list(range(3, 35))
                src, dst = H3, None
                sk, dk = "H3_%d", "OUT_%d"
            ngrp = len(tiles) // 2
            st = {"ty": None, "gty": None}

            def ginfo(ig):
                pair = tiles[2 * ig:2 * ig + 2]
                return pair, (1 if pair[0] in CT else 0), ig % 2

            def Ae(ig):
                pair, ty, ub = ginfo(ig)
                if ty != st["ty"]:
                    load_mod(SH, (l * 2 + ty) * 6 + 3, "SH")
                    load_mod(GS, (l * 2 + ty) * 6 + 4, "GS")
                    st["ty"] = ty
                for i, t in enumerate(pair):
                    xs = (2 * ig + i) % 6
                    P.dma("sync", lambda e, xs=xs, t=t: e.dma_start(out=X[xs], in_=src[t][:, :]),
                          reads=[sk % t], writes=["X%d" % xs])
                    norm_mod(X[xs], "X%d" % xs, JK, SS[i], RS[i], T1, U[i], GS, SH, str(i), ukey="U%d" % i)

            PTR2 = PS[6][:, :].bitcast(BF16)

            def Atr(ig):
                pair, ty, ub = ginfo(ig)
                for i, t in enumerate(pair):
                    ptr, pkey = (PTR, "PTR") if i == 0 else (PTR2, "PS6")
                    for c in range(8):
                        P.op("pe", lambda e, c=c, i=i, ptr=ptr: e.transpose(out=ptr[:, c * 128:(c + 1) * 128],
                                                                            in_=U[i][:, c * 128:(c + 1) * 128], identity=ID),
                             reads=["U%d" % i, "ID"], writes=[pkey], inc=(c == 7))
                for i, t in enumerate(pair):
                    ptr, pkey = (PTR, "PTR") if i == 0 else (PTR2, "PS6")
                    P.op("act", lambda e, i=i, ptr=ptr: e.copy(out=UT2[ub][:, :, i * 128:(i + 1) * 128],
                                                               in_=ptr[:, :].rearrange("p (k n) -> p k n", k=8)),
                         reads=[pkey], writes=["UT2_%d" % ub])

            def up(ig):
                pair, ty, ub = ginfo(ig)
                for jj in range(16):
                    hb = jj % 2
                    for j2 in range(2):
                        j = 2 * jj + j2
                        for k in range(8):
                            P.op("pe", lambda e, hb=hb, j2=j2, j=j, k=k: e.matmul(
                                PS[hb][:, j2 * 256:(j2 + 1) * 256], lhsT=W1[:, k, j * 128:(j + 1) * 128],
                                rhs=UT2[ub][:, k, :], start=(k == 0), stop=(k == 7)),
                                reads=[w1key(k, j * 128), "UT2_%d" % ub], writes=["PS%d" % hb], inc=(k == 7 and j2 == 1))
                    P.op("act", lambda e, hb=hb: e.activation(out=R[hb], in_=PS[hb][:, :], func=AF.Relu),
                         reads=["PS%d" % hb], writes=["R%d" % hb])
                    P.op("dve", lambda e, hb=hb, jj=jj: e.tensor_tensor(out=HT[:, jj * 512:(jj + 1) * 512], in0=R[hb], in1=R[hb], op=ALU.mult),
                         reads=["R%d" % hb], writes=["HT%d" % jj])

            def down(ig):
                pair, ty, ub = ginfo(ig)
                if ty != st["gty"]:
                    load_mod(GA, (l * 2 + ty) * 6 + 5, "GA")
                    st["gty"] = ty
                for i, t in enumerate(pair):
                    xs = (2 * ig + i) % 6
                    for g in range(2):
                        bk = 2 + i * 2 + g
                        for j in range(32):
                            P.op("pe", lambda e, bk=bk, j=j, i=i, g=g: e.matmul(
                                PS[bk][:, :], lhsT=HT[:, j * 256 + i * 128:j * 256 + (i + 1) * 128],
                                rhs=W2[:, j, g * 512:(g + 1) * 512], start=(j == 0), stop=(j == 31)),
                                reads=["HT%d" % (j // 2), "W2_%d" % (j // 8)], writes=["PS%d" % bk], inc=(j == 31))
                        P.op("dve", lambda e, bk=bk, g=g: e.tensor_tensor(out=T2[:, g * 512:(g + 1) * 512], in0=PS[bk][:, :],
                                                                          in1=GA[:, g * 512:(g + 1) * 512], op=ALU.mult),
                             reads=["PS%d" % bk, "GA"], writes=["T2"])
                    P.op("pool", lambda e, xs=xs: e.tensor_tensor(out=X[xs], in0=T2, in1=X[xs], op=ALU.add),
                         reads=["T2", "X%d" % xs], writes=["X%d" % xs])
                    if dst is not None:
                        P.dma("pool", lambda e, xs=xs, t=t: e.dma_start(out=dst[t][:, :], in_=X[xs]),
                              reads=["X%d" % xs], writes=[dk % t])
                    else:
                        P.dma("pool", lambda e, xs=xs, t=t: e.dma_start(out=outd[t - 3][:, :], in_=X[xs]),
                              reads=["X%d" % xs], writes=[dk % t])

            Ae(0)
            Atr(0)
            for ig in range(ngrp):
                if ig + 1 < ngrp:
                    Ae(ig + 1)
                up(ig)
                if ig + 1 < ngrp:
                    Atr(ig + 1)
                down(ig)
            P.barrier()

        plan = [(1, lambda: phase1(0)), (2, lambda: phase2(0)), (3, lambda: phase3(0)),
                (4, lambda: phase1(1)), (5, lambda: phase2(1)), (6, lambda: phase3(1))]
        for lvl, f in plan:
            if upto >= lvl:
                f()
        P.finish()

        block = es.enter_context(nc.Block())

        @block.sync
        def _(e):
            P.emit("sync", e)

        @block.scalar
        def _(e):
            P.emit("act", e)

        @block.gpsimd
        def _(e):
            P.emit("pool", e)

        @block.vector
        def _(e):
            P.emit("dve", e)

        @block.tensor
        def _(e):
            P.emit("pe", e)
    return nc


def _na_bias_tile(rpb, r_b, d):
    ka = np.arange(2)[:, None, None, None]
    kc = np.arange(64)[None, :, None, None]
    qa = np.arange(2)[None, None, :, None]
    qc = np.arange(64)[None, None, None, :]
    kr = r_b + 2 * d + ka
    qr = r_b + qa
    r0 = np.clip(qr - 4, 0, ROWS - 8)
    c0 = np.clip(qc - 8, 0, GRID_W - 16)
    valid = (kr >= r0) & (kr < r0 + 8) & (kr >= 0) & (kr < ROWS) & (qr >= 0) & (qr < ROWS) & (kc >= c0) & (kc < c0 + 16)
    ri = np.clip(kr - qr + 7, 0, 14)
    ci = np.clip(kc - qc + 15, 0, 30)
    ri, ci, valid = np.broadcast_arrays(ri, ci, valid)
    vals = rpb[:, ri, ci]
    out = np.where(valid[None], vals, np.float32(NEG)).astype(np.float32)
    return np.ascontiguousarray(out.reshape(H, 128, 128).transpose(1, 0, 2))


def _prep_core(b, half, inp):
    R0 = half * 64
    x = np.asarray(inp["x"])[b].reshape(ROWS, GRID_W, D)
    xe = np.zeros((NT, 128, D), np.float32)
    g0 = R0 - 6
    lo = max(g0, 0)
    hi = min(g0 + 76, ROWS)
    ext = np.zeros((76, GRID_W, D), np.float32)
    ext[lo - g0:hi - g0] = x[lo:hi]
    xe[:NLAT] = ext.reshape(NLAT, 128, D)
    xe[NLAT:] = np.asarray(inp["ctx"])[b].reshape(2, 128, D)
    cv = np.stack([np.asarray(inp["c"])[b], np.asarray(inp["c_ctx"])], 0)
    cT = np.ascontiguousarray(cv.reshape(2, 8, 128).transpose(2, 0, 1).reshape(128, 16))
    rpb = np.asarray(inp["na_rpb"])[0]
    bint = np.stack([_na_bias_tile(rpb, 64, d) for d in range(-2, 3)], 2)
    bint = np.ascontiguousarray(bint.reshape(128, H * 640))
    bs = []
    for t in SPECIAL:
        r_b = g0 + 2 * t
        bt = np.stack([_na_bias_tile(rpb, r_b, d) for d in range(-3, 4)], 2)
        bs.append(bt.reshape(128, H * 896))
    bspec = np.ascontiguousarray(np.stack(bs, 0))
    a = np.arange(128)[:, None]
    q = np.arange(128)[None, :]
    m_prev = np.where(q <= a, 0.0, NEG).astype(np.float32)
    m_mid = np.zeros((128, 128), np.float32)
    m_next = np.where(a <= q, 0.0, NEG).astype(np.float32)
    m_off = np.full((128, 128), NEG, np.float32)
    normal = np.concatenate([m_prev, m_mid, m_next], 1)
    first = np.concatenate([m_off if half == 0 else m_prev, m_mid, m_next], 1)
    last = np.concatenate([m_prev, m_mid, m_off if half == 1 else m_next], 1)
    mswa = np.ascontiguousarray(np.stack([normal, first, last], 0))
    tok = g0 * GRID_W + np.arange(NLAT * 128)
    row = (tok // GRID_W).astype(np.float32)
    col = (tok % GRID_W).astype(np.float32)
    inv = (np.float32(10000.0) ** (-np.arange(16, dtype=np.float32) / np.float32(16))).astype(np.float32)
    ang = np.stack([row[:, None] * inv, col[:, None] * inv], 1).astype(np.float32)
    cs = np.zeros((NT, 128, 64), np.float32)
    cs[:NLAT, :, 0:32] = np.cos(ang).reshape(NLAT, 128, 32)
    cs[:NLAT, :, 32:64] = np.sin(ang).reshape(NLAT, 128, 32)
    return {"xext": xe, "cT": cT, "bint": bint, "bspec": bspec, "mswa": mswa, "cs": cs}


def make_in_maps(inp):
    f = lambda k: np.ascontiguousarray(np.asarray(inp[k], dtype=np.float32))
    shared = {
        "ada_w": f("ada_w"), "ada_b": f("ada_b"), "g_mix": f("g_mix"), "g_mlp": f("g_mlp"),
        "mlp_w1": f("mlp_w1"), "mlp_w2": f("mlp_w2"),
        "wqkv0": f("na_wqkv")[0], "wqkv1": f("swa_wqkv")[0], "wo0": f("na_wo")[0], "wo1": f("swa_wo")[0],
        "gains": np.ascontiguousarray(np.stack([f("na_q_gain")[0], f("na_k_gain")[0], f("swa_q_gain")[0], f("swa_k_gain")[0]], 0)),
        "sink": f("swa_sink").reshape(1, H),
        "ident": np.eye(128, dtype=np.float32),
    }
    maps = []
    for core in range(8):
        b, half = core // 2, core % 2
        m = dict(shared)
        m.update(_prep_core(b, half, inp))
        maps.append(m)
    return maps


_NC_CACHE = {}


def kernel(**inputs):
    if "nc" not in _NC_CACHE:
        _NC_CACHE["nc"] = build_nc()
    nc = _NC_CACHE["nc"]
    maps = make_in_maps(inputs)
    res = run_bass_kernel_spmd(nc, maps, core_ids=list(range(8)))
    out = np.zeros((4, 8192, D), np.float32)
    for core in range(8):
        b, half = core // 2, core % 2
        out[b, half * 4096:(half + 1) * 4096] = np.asarray(res.results[core]["out"]).reshape(4096, D)
    return out
```

```python
import numpy as np
from contextlib import ExitStack
import concourse.bass as bass
import concourse.mybir as mybir
from concourse.bass_utils import run_bass_kernel_spmd

F32 = mybir.dt.float32
BF16 = mybir.dt.bfloat16
AF = mybir.ActivationFunctionType
ALU = mybir.AluOpType
AX = mybir.AxisListType

D = 1024
H = 16
DH = 64
KV = 4
DFF = 4096
GRID_W = 64
ROWS = 128
CTX = 256
EPS = 1e-6
NEG = -200.0
NLAT = 38
NT = 40
CT = (38, 39)
SPECIAL = (3, 4, 33, 34)
ARENA_WORDS = 53000


class Prog:
    ND = 48

    def __init__(self, nc, es):
        self.nc = nc
        self.engs = ["sync", "act", "pool", "dve", "pe"]
        self.q = {e: [] for e in self.engs}
        self.cnt = {e: 0 for e in self.engs}
        self.sem = {e: es.enter_context(nc.semaphore("s_" + e)) for e in ["act", "pool", "dve", "pe"]}
        self.dsem = {q: [es.enter_context(nc.semaphore("d_%s%d" % (q, i))) for i in range(self.ND)]
                     for q in ["sync", "pool"]}
        self.dcnt = {q: [0] * self.ND for q in ["sync", "pool"]}
        self.drr = {q: 0 for q in ["sync", "pool"]}
        self.lastw = {}
        self.readers = {}
        self.pending_barrier = {e: None for e in self.engs}
        self.nops = 0

    def _deps(self, eng, reads, writes):
        deps = {}

        def add(t):
            if t is None:
                return
            k, v = t
            if deps.get(k, 0) < v:
                deps[k] = v
        for k in reads:
            add(self.lastw.get(k))
        for k in writes:
            add(self.lastw.get(k))
            for t in self.readers.get(k, {}).items():
                add(t)
        pb = self.pending_barrier[eng]
        if pb is not None:
            for t in pb.items():
                add(t)
            self.pending_barrier[eng] = None
        own = ("c", eng)
        if own in deps:
            if eng == "pe":
                del deps[own]
            elif deps[own] > self.cnt[eng]:
                deps[own] = self.cnt[eng]
                if deps[own] == 0:
                    del deps[own]
        return deps

    def _note(self, ticket, reads, writes):
        k0, v0 = ticket
        for k in reads:
            r = self.readers.setdefault(k, {})
            if r.get(k0, 0) < v0:
                r[k0] = v0
        for k in writes:
            self.lastw[k] = ticket
            self.readers[k] = {}

    def op(self, eng, fn, reads=(), writes=(), inc=True):
        deps = self._deps(eng, reads, writes)
        if inc:
            self.cnt[eng] += 1
            ticket = (("c", eng), self.cnt[eng])
        else:
            ticket = (("c", eng), self.cnt[eng] + 1)
        self.q[eng].append((fn, deps, ("c", inc)))
        self._note(ticket, reads, writes)
        self.nops += 1
        return ticket

    def dma(self, queue, fn, reads=(), writes=()):
        deps = self._deps(queue, reads, writes)
        i = self.drr[queue]
        self.drr[queue] = (i + 1) % self.ND
        if self.dcnt[queue][i]:
            k = ("d", queue, i)
            if deps.get(k, 0) < self.dcnt[queue][i]:
                deps[k] = self.dcnt[queue][i]
        self.dcnt[queue][i] += 16
        ticket = (("d", queue, i), self.dcnt[queue][i])
        self.q[queue].append((fn, deps, ("d", i)))
        self._note(ticket, reads, writes)
        self.nops += 1
        return ticket

    def snapshot(self):
        snap = {}
        for e in ["act", "pool", "dve", "pe"]:
            if self.cnt[e]:
                snap[("c", e)] = self.cnt[e]
        for q in ["sync", "pool"]:
            for i in range(self.ND):
                if self.dcnt[q][i]:
                    snap[("d", q, i)] = self.dcnt[q][i]
        return snap

    def barrier(self):
        snap = self.snapshot()
        for e in self.engs:
            self.pending_barrier[e] = dict(snap)
        self.lastw = {}
        self.readers = {}

    def finish(self):
        snap = self.snapshot()
        self.q["sync"].append((None, snap, None))

    def _semh(self, k):
        if k[0] == "c":
            return self.sem[k[1]]
        return self.dsem[k[1]][k[2]]

    def emit(self, eng, e):
        known = {}
        for fn, deps, kind in self.q[eng]:
            for k, v in deps.items():
                if known.get(k, 0) >= v:
                    continue
                e.wait_ge(self._semh(k), v)
                known[k] = v
            if fn is None:
                continue
            ins = fn(e)
            if kind[0] == "c":
                if kind[1]:
                    ins.then_inc(self.sem[eng], 1)
            else:
                ins.then_inc(self.dsem[eng][kind[1]], 16)


class Arena:
    def __init__(self, ar, nwords):
        self.ar = ar
        self.n = nwords
        self.off = 0
        self.base = 0

    def mark(self):
        self.base = self.off

    def reset(self):
        self.off = self.base

    def f32(self, n):
        a = self.ar[:, self.off:self.off + n]
        self.off += n
        assert self.off <= self.n, ("arena overflow", self.off, self.n)
        return a

    def bf16(self, n):
        w = (n + 1) // 2
        a = self.ar[:, self.off:self.off + w].bitcast(BF16)
        self.off += w
        assert self.off <= self.n, ("arena overflow", self.off, self.n)
        return a


def build_nc(debug=False, upto=99):
    nc = bass.Bass("TRN2", target_bir_lowering=False)
    skind = "ExternalOutput" if debug else "Internal"

    def din(name, shape, dt=F32):
        return nc.dram_tensor(name, list(shape), dt, kind="ExternalInput").ap()

    def dscr(name, shape, dt):
        return nc.dram_tensor(name, list(shape), dt, kind=skind).ap()

    xext = din("xext", [NT, 128, D])
    cT = din("cT", [128, 16])
    ada_w = din("ada_w", [2, D, 6 * D])
    ada_b = din("ada_b", [2, 6 * D])
    g_mix = din("g_mix", [2, D])
    g_mlp = din("g_mlp", [2, D])
    w1d = din("mlp_w1", [2, D, DFF])
    w2d = din("mlp_w2", [2, DFF, D])
    wqkv0 = din("wqkv0", [D, 3 * D])
    wqkv1 = din("wqkv1", [D, D + 2 * KV * DH])
    wo0 = din("wo0", [D, D])
    wo1 = din("wo1", [D, D])
    gains = din("gains", [4, DH])
    sink = din("sink", [1, H])
    bint = din("bint", [128, H * 640])
    bspec = din("bspec", [4, 128, H * 896])
    mswa = din("mswa", [3, 128, 384])
    csd = din("cs", [NT, 128, 64])
    identd = din("ident", [128, 128])
    outd = nc.dram_tensor("out", [32, 128, D], F32, kind="ExternalOutput").ap()

    MODS = dscr("MODS", [24, 128, D], F32)
    QTd = dscr("QTd", [NT, 128, D], BF16)
    KTd = dscr("KTd", [NT, 128, D], BF16)
    VAd = dscr("VAd", [NT, 128, H * 65], BF16)
    H1 = dscr("H1", [NT, 128, D], F32)
    H2 = dscr("H2", [NT, 128, D], F32)
    H3 = dscr("H3", [NT, 128, D], F32)

    es = ExitStack()
    with es:
        AR = es.enter_context(nc.sbuf_tensor("AR", [128, ARENA_WORDS], F32))
        PS = [es.enter_context(nc.psum_tensor("PS%d" % i, [128, 512], F32)) for i in range(7)]
        PTR = es.enter_context(nc.psum_tensor("PTR", [128, 1024], BF16))
        P = Prog(nc, es)
        A = Arena(AR, ARENA_WORDS)

        ID = A.bf16(128)
        P.dma("pool", lambda e: e.dma_start(out=ID, in_=identd[:, :]), writes=["ID"])
        A.mark()

        def transposes(src, srckey, n, dst_fn=None):
            for i in range(n):
                P.op("pe", lambda e, i=i: e.transpose(out=PTR[:, i * 128:(i + 1) * 128],
                                                      in_=src[:, i * 128:(i + 1) * 128], identity=ID),
                     reads=[srckey, "ID"], writes=["PTR"], inc=(i == n - 1))

        def norm_mod(X, xkey, JK, SS, RS, T1, U, GS, SH, tag, ukey="U"):
            P.op("act", lambda e: e.activation(out=JK, in_=X, func=AF.Square, accum_out=SS),
                 reads=[xkey], writes=["JK", "SS" + tag])
            P.op("act", lambda e: e.activation(out=RS, in_=SS, func=AF.Sqrt, bias=EPS, scale=1.0 / D),
                 reads=["SS" + tag], writes=["RS" + tag])
            P.op("dve", lambda e: e.reciprocal(out=RS, in_=RS), reads=["RS" + tag], writes=["RS" + tag])
            P.op("dve", lambda e: e.scalar_tensor_tensor(out=T1, in0=X, scalar=RS, in1=GS,
                                                         op0=ALU.mult, op1=ALU.mult),
                 reads=[xkey, "RS" + tag, "GS"], writes=["T1"])
            P.op("pool", lambda e: e.tensor_tensor(out=U, in0=T1, in1=SH, op=ALU.add),
                 reads=["T1", "SH"], writes=[ukey])

        def load_w_cast(dst3, src2, nk, keyp, ncols, split=1):
            sv = src2.rearrange("(k p) n -> p k n", p=128)
            cw = ncols // split
            for s in range(split):
                P.dma("pool", lambda e, s=s: e.dma_start(out=dst3[:, :, s * cw:(s + 1) * cw],
                                                         in_=sv[:, :, s * cw:(s + 1) * cw]),
                      writes=["%s_%d" % (keyp, s)])
            return lambda k, col: "%s_%d" % (keyp, col // cw)

        if upto >= 0:
            A.reset()
            CTs = A.f32(16)
            SIL = A.f32(16)
            SB = A.bf16(16 * 128).rearrange("p (r k m) -> p r k m", r=2, k=8)
            WA = [A.bf16(8 * 512).rearrange("p (k n) -> p k n", k=8) for _ in range(4)]
            BB = [A.f32(512) for _ in range(4)]
            GG = [A.f32(512) for _ in range(4)]
            TMP = [A.f32(512) for _ in range(2)]
            OUT = [A.f32(512) for _ in range(4)]
            P.dma("sync", lambda e: e.dma_start(out=CTs, in_=cT[:, :]), writes=["CT"])
            P.op("act", lambda e: e.activation(out=SIL, in_=CTs, func=AF.Silu), reads=["CT"], writes=["SIL"])
            P.op("dve", lambda e: e.tensor_copy(out=SB.rearrange("p r k m -> p (r k) m"),
                                                in_=SIL.unsqueeze(2).broadcast_to([128, 16, 128])),
                 reads=["SIL"], writes=["SB"])
            it = 0
            for l in range(2):
                awv = ada_w[l].rearrange("(k p) n -> p k n", p=128)
                for g in range(12):
                    j = g // 2
                    half = g % 2
                    kind = j % 3
                    b = it % 4
                    P.dma("pool", lambda e, b=b, g=g, awv=awv: e.dma_start(
                        out=WA[b][:, :, :], in_=awv[:, :, g * 512:(g + 1) * 512]), writes=["WA%d" % b])
                    P.dma("sync", lambda e, b=b, l=l, g=g: e.dma_start(
                        out=BB[b], in_=ada_b[l:l + 1, g * 512:(g + 1) * 512].partition_broadcast(128)),
                        writes=["BB%d" % b])
                    if kind == 1:
                        gsrc = g_mix if j == 1 else g_mlp
                        P.dma("sync", lambda e, b=b, l=l, half=half, gsrc=gsrc: e.dma_start(
                            out=GG[b], in_=gsrc[l:l + 1, half * 512:(half + 1) * 512].partition_broadcast(128)),
                            writes=["GG%d" % b])
                    for ty in range(2):
                        pb = (it % 2) * 2 + ty
                        for k in range(8):
                            P.op("pe", lambda e, pb=pb, ty=ty, k=k, b=b: e.matmul(
                                PS[pb][:, :], lhsT=SB[:, ty, k, :], rhs=WA[b][:, k, :], start=(k == 0), stop=(k == 7)),
                                reads=["SB", "WA%d" % b], writes=["PS%d" % pb], inc=(k == 7))
                        ob = (it % 2) * 2 + ty
                        okey = "OUT%d" % ob
                        if kind == 1:
                            P.op("dve", lambda e, pb=pb, b=b: e.tensor_tensor(out=TMP[b % 2], in0=PS[pb][:, :], in1=BB[b], op=ALU.add),
                                 reads=["PS%d" % pb, "BB%d" % b], writes=["TMP%d" % (b % 2)])
                            P.op("dve", lambda e, ob=ob, b=b: e.scalar_tensor_tensor(
                                out=OUT[ob], in0=TMP[b % 2], scalar=1.0, in1=GG[b], op0=ALU.add, op1=ALU.mult),
                                reads=["TMP%d" % (b % 2), "GG%d" % b], writes=[okey])
                        else:
                            P.op("dve", lambda e, pb=pb, b=b, ob=ob: e.tensor_tensor(out=OUT[ob], in0=PS[pb][:, :], in1=BB[b], op=ALU.add),
                                 reads=["PS%d" % pb, "BB%d" % b], writes=[okey])
                        mi = (l * 2 + ty) * 6 + j
                        P.dma("sync", lambda e, ob=ob, mi=mi, half=half: e.dma_start(
                            out=MODS[mi][:, half * 512:(half + 1) * 512], in_=OUT[ob]),
                            reads=[okey], writes=["MODS%d_%d" % (mi, half)])
                    it += 1
            P.barrier()

        def load_mod(dst, mi, key):
            P.dma("sync", lambda e: e.dma_start(out=dst, in_=MODS[mi][:, :]),
                  reads=["MODS%d_0" % mi, "MODS%d_1" % mi], writes=[key])

        def phase1(l):
            A.reset()
            nqkv = 3 * D if l == 0 else D + 2 * KV * DH
            ng = nqkv // 512
            hk = H if l == 0 else KV
            W = A.bf16(8 * nqkv).rearrange("p (k n) -> p k n", k=8)
            GS = A.f32(D)
            SH = A.f32(D)
            GQ = A.f32(DH)
            GK = A.f32(DH)
            GQc = A.f32(1)
            GKc = A.f32(1)
            X = [A.f32(D) for _ in range(3)]
            T1 = A.f32(D)
            JK = A.bf16(D)
            SS = [A.f32(1) for _ in range(2)]
            RS = [A.f32(1) for _ in range(2)]
            U = A.bf16(D)
            UT = [A.bf16(D) for _ in range(2)]
            SQ = [A.f32(512) for _ in range(4)]
            SSQ = A.f32(32)
            RQ = A.f32(32)
            QN = A.f32(D)
            KN = A.f32(D)
            QH = [A.bf16(D) for _ in range(2)]
            KH = [A.bf16(D) for _ in range(2)]
            QT = [A.bf16(D) for _ in range(2)]
            KT = [A.bf16(D) for _ in range(2)]
            VA = [A.bf16(H * 65) for _ in range(2)]
            if l == 1:
                RA = [A.f32(512) for _ in range(2)]
                RB = [A.f32(512) for _ in range(2)]
                NR = 34
                CSall = A.f32(NR * 64).rearrange("p (t c) -> p t c", c=64)
                TBqA = A.f32(NR * 128).rearrange("p (t c) -> p t c", c=128)
                TBkA = A.f32(NR * 128).rearrange("p (t c) -> p t c", c=128)
            PTRQ = PS[6][:, :].bitcast(BF16)
            wsrc = wqkv0 if l == 0 else wqkv1
            wkey = load_w_cast(W, wsrc, 8, "W", nqkv, split=ng)
            P.dma("sync", lambda e: e.dma_start(out=GQ, in_=gains[2 * l:2 * l + 1, :].partition_broadcast(128)), writes=["GQ"])
            P.dma("sync", lambda e: e.dma_start(out=GK, in_=gains[2 * l + 1:2 * l + 2, :].partition_broadcast(128)), writes=["GK"])
            P.op("dve", lambda e: e.tensor_scalar_mul(out=GQ, in0=GQ, scalar1=0.125), reads=["GQ"], writes=["GQ"])
            if l == 0:
                for hp in range(2):
                    P.dma("sync", lambda e, hp=hp: e.dma_start(out=GQc[hp * 64:(hp + 1) * 64, :],
                                                              in_=gains[0:1, :].rearrange("o d -> d o")), writes=["GQc%d" % hp])
                    P.dma("sync", lambda e, hp=hp: e.dma_start(out=GKc[hp * 64:(hp + 1) * 64, :],
                                                              in_=gains[1:2, :].rearrange("o d -> d o")), writes=["GKc%d" % hp])
                P.op("dve", lambda e: e.tensor_scalar_mul(out=GQc, in0=GQc, scalar1=0.125),
                     reads=["GQc0", "GQc1"], writes=["GQc"])
            for b in range(2):
                P.op("pool", lambda e, b=b: e.memset(VA[b], 1.0), writes=["VA%d" % b])
            if l == 1:
                P.dma("sync", lambda e: e.dma_start(out=CSall, in_=csd[2:2 + NR].rearrange("t p c -> p t c")), writes=["CSall"])
                cosv = CSall[:, :, 0:32].rearrange("p t (a f) -> p t a f", a=2)
                sinv = CSall[:, :, 32:64].rearrange("p t (a f) -> p t a f", a=2)
                for (TBA, G_, gk_, tk_, eng_) in ((TBqA, GQ, "GQ", "TBqA", "pool"), (TBkA, GK, "GK", "TBkA", "dve")):
                    g4 = G_.rearrange("p (a s f) -> p a s f", a=2, s=2)
                    tb4 = TBA.rearrange("p t (i a f) -> p t i a f", i=4, a=2)
                    for i_, (cs_, sidx) in enumerate(((cosv, 0), (sinv, 1), (sinv, 0), (cosv, 1))):
                        P.op(eng_, lambda e, i_=i_, cs_=cs_, sidx=sidx, g4=g4, tb4=tb4: e.tensor_tensor(
                            out=tb4[:, :, i_, :, :], in0=cs_,
                            in1=g4[:, :, sidx, :].unsqueeze(1).broadcast_to([128, NR, 2, 16]), op=ALU.mult),
                            reads=["CSall", gk_], writes=[tk_])
            tiles = list(range(NLAT)) if l == 0 else list(range(2, 36))
            tiles = tiles + list(CT)
            n = len(tiles)
            src = xext if l == 0 else H2
            st = {"ty": None}
            gk_b = GK.unsqueeze(1).broadcast_to([128, hk, DH])

            def info(it):
                t = tiles[it]
                ty = 1 if t in CT else 0
                return t, ty, it % 2, it % 3, (l == 1 and ty == 0), not (l == 1 and ty == 1)

            def pbank(it, g):
                return g if l == 0 else (it % 2) * 3 + g

            def stageA(it):
                t, ty, b2, xs, rope, need_q = info(it)
                if ty != st["ty"]:
                    load_mod(SH, (l * 2 + ty) * 6 + 0, "SH")
                    load_mod(GS, (l * 2 + ty) * 6 + 1, "GS")
                    st["ty"] = ty
                xk = "X%d" % xs
                skey = ("H2_%d" % t) if l == 1 else None
                P.dma("sync", lambda e: e.dma_start(out=X[xs], in_=src[t][:, :]),
                      reads=([skey] if skey else []), writes=[xk])
                norm_mod(X[xs], xk, JK, SS[b2], RS[b2], T1, U, GS, SH, str(b2))

            def stageAtr(it):
                t, ty, b2, xs, rope, need_q = info(it)
                transposes(U, "U", 8)
                P.op("act", lambda e: e.copy(out=UT[b2], in_=PTR[:, :]), reads=["PTR"], writes=["UT%d" % b2])

            def stageB(it):
                t, ty, b2, xs, rope, need_q = info(it)
                for g in range(ng):
                    bk = pbank(it, g)
                    for k in range(8):
                        P.op("pe", lambda e, g=g, k=k, bk=bk: e.matmul(PS[bk][:, :], lhsT=UT[b2][:, k * 128:(k + 1) * 128],
                                                                      rhs=W[:, k, g * 512:(g + 1) * 512],
                                                                      start=(k == 0), stop=(k == 7)),
                             reads=["UT%d" % b2, wkey(k, g * 512)], writes=["PS%d" % bk], inc=(k == 7))

            def rope_ops(srcT, skey, nh, outs, okey, TB, tkey):
                s5 = srcT.rearrange("p (h a s f) -> p h a s f", a=2, s=2, f=16)
                x1 = s5[:, :, :, 0, :]
                x2 = s5[:, :, :, 1, :]
                tb3 = TB.rearrange("p (i a f) -> p i a f", i=4, a=2)
                tbb = [tb3[:, i_, :, :].unsqueeze(1).broadcast_to([128, nh, 2, 16]) for i_ in range(4)]
                ra = [RA[i][:, 0:nh * 32].rearrange("p (h a f) -> p h a f", a=2, f=16) for i in range(2)]
                rb = [RB[i][:, 0:nh * 32].rearrange("p (h a f) -> p h a f", a=2, f=16) for i in range(2)]
                P.op("dve", lambda e: e.tensor_tensor(out=ra[0], in0=x1, in1=tbb[0], op=ALU.mult), reads=[skey, tkey], writes=["RA0"])
                P.op("pool", lambda e: e.tensor_tensor(out=rb[0], in0=x2, in1=tbb[1], op=ALU.mult), reads=[skey, tkey], writes=["RB0"])
                P.op("dve", lambda e: e.tensor_tensor(out=ra[1], in0=x1, in1=tbb[2], op=ALU.mult), reads=[skey, tkey], writes=["RA1"])
                P.op("pool", lambda e: e.tensor_tensor(out=rb[1], in0=x2, in1=tbb[3], op=ALU.mult), reads=[skey, tkey], writes=["RB1"])
                for o in outs:
                    P.op("dve", lambda e, o=o: e.tensor_tensor(out=o[:, :, :, 0, :], in0=ra[0], in1=rb[0], op=ALU.subtract),
                         reads=["RA0", "RB0"], writes=[okey])
                    P.op("pool", lambda e, o=o: e.tensor_tensor(out=o[:, :, :, 1, :], in0=ra[1], in1=rb[1], op=ALU.add),
                         reads=["RA1", "RB1"], writes=[okey])

            def stageC(it):
                t, ty, b2, xs, rope, need_q = info(it)
                qk = "QH%d" % b2
                kk = "KH%d" % b2
                segs = []
                if need_q:
                    segs += [(pbank(it, 0), 512, 0, 8, "q"), (pbank(it, 1), 512, 8, 8, "q")]
                if l == 0:
                    segs += [(2, 512, 16, 8, "k"), (3, 512, 24, 8, "k")]
                else:
                    segs += [(pbank(it, 2), 256, 16, 4, "k")]
                for i, (bk, ncol, s0, nh, kind) in enumerate(segs):
                    P.op("act", lambda e, bk=bk, ncol=ncol, i=i: e.activation(
                        out=SQ[i][:, 0:ncol], in_=PS[bk][:, 0:ncol], func=AF.Square),
                        reads=["PS%d" % bk], writes=["SQ%d" % i])
                for i, (bk, ncol, s0, nh, kind) in enumerate(segs):
                    P.op("dve", lambda e, ncol=ncol, s0=s0, nh=nh, i=i: e.tensor_reduce(
                        out=SSQ[:, s0:s0 + nh], in_=SQ[i][:, 0:ncol].rearrange("p (h d) -> p h d", d=DH),
                        axis=AX.X, op=ALU.add), reads=["SQ%d" % i], writes=["SSQ"])
                if l == 0:
                    for i2, bk in enumerate((4, 5)):
                        P.op("act", lambda e, i2=i2, bk=bk: e.copy(
                            out=VA[b2].rearrange("p (h d) -> p h d", d=65)[:, i2 * 8:(i2 + 1) * 8, 0:64],
                            in_=PS[bk][:, :].rearrange("p (h d) -> p h d", d=DH)),
                            reads=["PS%d" % bk], writes=["VA%d" % b2])
                    vcols = H * 65
                else:
                    bkv = pbank(it, 2)
                    P.op("act", lambda e: e.copy(
                        out=VA[b2][:, 0:KV * 65].rearrange("p (h d) -> p h d", d=65)[:, :, 0:64],
                        in_=PS[bkv][:, 256:512].rearrange("p (h d) -> p h d", d=DH)),
                        reads=["PS%d" % bkv], writes=["VA%d" % b2])
                    vcols = KV * 65
                P.dma("sync", lambda e: e.dma_start(out=VAd[t][:, 0:vcols], in_=VA[b2][:, 0:vcols]),
                      reads=["VA%d" % b2], writes=["VAd%d" % t])
                c_lo = 0 if need_q else 16
                c_hi = 32 if l == 0 else 20
                P.op("act", lambda e: e.activation(out=RQ[:, c_lo:c_hi], in_=SSQ[:, c_lo:c_hi], func=AF.Sqrt, bias=EPS, scale=1.0 / DH),
                     reads=["SSQ"], writes=["RQ"])
                P.op("dve", lambda e: e.reciprocal(out=RQ[:, c_lo:c_hi], in_=RQ[:, c_lo:c_hi]), reads=["RQ"], writes=["RQ"])
                for (bk, ncol, s0, nh, kind) in segs:
                    d0 = (s0 % 16) * DH
                    if l == 0:
                        dst, dkey = (QH[b2], qk) if kind == "q" else (KH[b2], kk)
                    else:
                        dst, dkey = (QN, "QN") if kind == "q" else (KN, "KN")
                    P.op("dve", lambda e, bk=bk, ncol=ncol, s0=s0, nh=nh, d0=d0, dst=dst: e.tensor_tensor(
                        out=dst[:, d0:d0 + ncol].rearrange("p (h d) -> p h d", d=DH),
                        in0=PS[bk][:, 0:ncol].rearrange("p (h d) -> p h d", d=DH),
                        in1=RQ[:, s0:s0 + nh].unsqueeze(2).broadcast_to([128, nh, DH]), op=ALU.mult),
                        reads=["PS%d" % bk, "RQ"], writes=[dkey])
                if l == 1:
                    khd = KH[b2][:, 0:KV * 128].rearrange("p (h r d) -> p h r d", h=KV, r=2)
                    if rope:
                        q5 = QH[b2].rearrange("p (h a s f) -> p h a s f", a=2, s=2, f=16)
                        rope_ops(QN, "QN", H, [q5], qk, TBqA[:, t - 2, :], "TBqA")
                        k5 = [khd[:, :, r, :].rearrange("p h (a s f) -> p h a s f", a=2, s=2) for r in range(2)]
                        rope_ops(KN[:, 0:KV * DH], "KN", KV, k5, kk, TBkA[:, t - 2, :], "TBkA")
                    else:
                        for r in range(2):
                            P.op("pool", lambda e, r=r: e.tensor_tensor(
                                out=khd[:, :, r, :], in0=KN[:, 0:KV * DH].rearrange("p (h d) -> p h d", d=DH),
                                in1=gk_b, op=ALU.mult), reads=["KN", "GK"], writes=[kk])

            def trq(srcT, srckey, nn):
                for i in range(nn):
                    P.op("pe", lambda e, i=i: e.transpose(out=PTRQ[:, i * 128:(i + 1) * 128],
                                                          in_=srcT[:, i * 128:(i + 1) * 128], identity=ID),
                         reads=[srckey, "ID"], writes=["PS6"], inc=(i == nn - 1))

            def evac(dstT, dkey, ncols, gcol, gkey):
                if l == 0:
                    P.op("act", lambda e: e.activation(out=dstT[:, 0:ncols], in_=PTRQ[:, 0:ncols], func=AF.Copy, scale=gcol[:, 0:1]),
                         reads=["PS6"] + gkey, writes=[dkey])
                else:
                    P.op("act", lambda e: e.copy(out=dstT[:, 0:ncols], in_=PTRQ[:, 0:ncols]), reads=["PS6"], writes=[dkey])

            def stageD(it):
                t, ty, b2, xs, rope, need_q = info(it)
                if need_q:
                    trq(QH[b2], "QH%d" % b2, 8)
                    evac(QT[b2], "QT%d" % b2, D, GQc, ["GQc"])
                    P.dma("sync", lambda e: e.dma_start(out=QTd[t][:, :], in_=QT[b2]),
                          reads=["QT%d" % b2], writes=["QTd%d" % t])
                nkt = 8 if l == 0 else KV
                trq(KH[b2], "KH%d" % b2, nkt)
                evac(KT[b2], "KT%d" % b2, nkt * 128, GKc, ["GKc0", "GKc1"])
                P.dma("sync", lambda e: e.dma_start(out=KTd[t][:, 0:nkt * 128], in_=KT[b2][:, 0:nkt * 128]),
                      reads=["KT%d" % b2], writes=["KTd%d" % t])

            stageA(0)
            stageAtr(0)
            for s in range(n + 1):
                if s + 1 < n:
                    stageA(s + 1)
                if s < n:
                    stageB(s)
                    stageC(s)
                if s + 1 < n:
                    stageAtr(s + 1)
                if s >= 1:
                    stageD(s - 1)
            P.barrier()

        def phase2(l):
            A.reset()
            hk = H if l == 0 else KV
            kcols = (8 if l == 0 else KV) * 128
            vcols = hk * 65
            NS = 8
            WO = A.bf16(8 * D).rearrange("p (k n) -> p k n", k=8)
            GA = A.f32(D)
            KR = [A.bf16(kcols) for _ in range(NS)]
            VR = [A.bf16(vcols) for _ in range(NS)]
            KC = [A.bf16(kcols) for _ in range(2)]
            VC = [A.bf16(vcols) for _ in range(2)]
            Q = [A.bf16(D) for _ in range(2)]
            HR = [A.f32(D) for _ in range(2)]
            T = [A.f32(896) for _ in range(2)]
            PT = [A.bf16(1152) for _ in range(6)]
            DEN = A.f32(H)
            RD = A.f32(H)
            AO = A.bf16(D)
            AOT = A.bf16(D)
            T2 = A.f32(D)
            HN = [A.f32(D) for _ in range(2)]
            if l == 0:
                BI = A.f32(H * 640)
                BS = [A.f32(896) for _ in range(2)]
                P.dma("sync", lambda e: e.dma_start(out=BI[:, 0:H * 320], in_=bint[:, 0:H * 320]), writes=["BIa"])
                P.dma("sync", lambda e: e.dma_start(out=BI[:, H * 320:], in_=bint[:, H * 320:]), writes=["BIb"])
            else:
                MS = A.f32(3 * 384)
                ESK = A.f32(H)
                P.dma("sync", lambda e: e.dma_start(out=MS.rearrange("p (v c) -> p v c", v=3),
                                                    in_=mswa.rearrange("v p c -> p v c")), writes=["MS"])
                P.dma("sync", lambda e: e.dma_start(out=ESK, in_=sink[0:1, :].partition_broadcast(128)), writes=["ESK"])
                P.op("act", lambda e: e.activation(out=ESK, in_=ESK, func=AF.Exp), reads=["ESK"], writes=["ESK"])
            wokey = load_w_cast(WO, wo0 if l == 0 else wo1, 8, "WO", D, split=2)
            for i, t in enumerate(CT):
                P.dma("sync", lambda e, i=i, t=t: e.dma_start(out=KC[i], in_=KTd[t][:, 0:kcols]),
                      reads=["KTd%d" % t], writes=["KC%d" % i])
                P.dma("sync", lambda e, i=i, t=t: e.dma_start(out=VC[i], in_=VAd[t][:, 0:vcols]),
                      reads=["VAd%d" % t], writes=["VC%d" % i])
            if l == 0:
                blocks = list(range(2, 36)) + list(CT)
                lo_t, hi_t = 0, NLAT - 1
            else:
                blocks = list(range(3, 35))
                lo_t, hi_t = 2, 35
            src = xext if l == 0 else H2
            dst = H1 if l == 0 else H3
            state = {"next": lo_t, "ty": None}

            def ensure_loaded(tmax):
                while state["next"] <= min(tmax, hi_t):
                    tt = state["next"]
                    s = tt % NS
                    P.dma("sync", lambda e, s=s, tt=tt: e.dma_start(out=KR[s], in_=KTd[tt][:, 0:kcols]),
                          reads=["KTd%d" % tt], writes=["KR%d" % s])
                    P.dma("sync", lambda e, s=s, tt=tt: e.dma_start(out=VR[s], in_=VAd[tt][:, 0:vcols]),
                          reads=["VAd%d" % tt], writes=["VR%d" % s])
                    state["next"] += 1

            binfo = {}

            def prologue(ib):
                t = blocks[ib]
                isctx = t in CT
                ty = 1 if isctx else 0
                spec = (l == 0 and t in SPECIAL)
                if isctx:
                    loc = []
                elif l == 0:
                    r = 3 if spec else 2
                    loc = list(range(t - r, t + r + 1))
                else:
                    loc = [t - 1, t, t + 1]
                if loc:
                    ensure_loaded(loc[-1] + 1)
                qb = ib % 2
                P.dma("sync", lambda e: e.dma_start(out=Q[qb], in_=QTd[t][:, :]),
                      reads=["QTd%d" % t], writes=["Q%d" % qb])
                skey = ("H2_%d" % t) if l == 1 else None
                P.dma("sync", lambda e: e.dma_start(out=HR[qb], in_=src[t][:, :]),
                      reads=([skey] if skey else []), writes=["HR%d" % qb])
                chunks = [(KR[tt % NS], VR[tt % NS], "KR%d" % (tt % NS), "VR%d" % (tt % NS)) for tt in loc]
                chunks += [(KC[i], VC[i], "KC%d" % i, "VC%d" % i) for i in range(2)]
                binfo[ib] = dict(t=t, ty=ty, spec=spec, nloc=len(loc), qb=qb, chunks=chunks)

            items = [(ib, h) for ib in range(len(blocks)) for h in range(H)]
            hinfo = {}

            def scores(s):
                ib, h = items[s]
                if h == 0:
                    prologue(ib)
                bi_ = binfo[ib]
                t, spec, nloc, qb, chunks = bi_["t"], bi_["spec"], bi_["nloc"], bi_["qb"], bi_["chunks"]
                nch = len(chunks)
                nlc = nloc * 128
                nbk = (nch + 3) // 4
                sb = 0 if nbk == 3 else (s % 2) * 2
                tb = s % 2
                pb = s % 6
                hinfo[s] = pb
                base = (h % 2) * 64
                qp = h // 2
                ks = (h // 2) if l == 0 else (h // 4)
                for c, (kc_, vc_, kk, vk) in enumerate(chunks):
                    bank = sb + c // 4
                    col = (c % 4) * 128
                    P.op("pe", lambda e, bank=bank, col=col, kc_=kc_: e.matmul(
                        PS[bank][:, col:col + 128], lhsT=kc_[base:base + 64, ks * 128:(ks + 1) * 128],
                        rhs=Q[qb][base:base + 64, qp * 128:(qp + 1) * 128], start=True, stop=True),
                        reads=[kk, "Q%d" % qb], writes=["PS%d" % bank], inc=(c == nch - 1 or c % 4 == 3))
                if nloc:
                    if l == 0:
                        if spec:
                            si = SPECIAL.index(t)
                            bsb = h % 2
                            P.dma("sync", lambda e: e.dma_start(
                                out=BS[bsb], in_=bspec[si][:, h * 896:(h + 1) * 896]), writes=["BS%d" % bsb])
                            bias = BS[bsb]
                            bkeys = ["BS%d" % bsb]
                        else:
                            bias = BI[:, h * 640:(h + 1) * 640]
                            bkeys = ["BIa", "BIb"]
                    else:
                        v = 1 if t == 3 else (2 if t == 34 else 0)
                        bias = MS[:, v * 384:(v + 1) * 384]
                        bkeys = ["MS"]
                    for bi in range(nbk):
                        a0 = bi * 512
                        a1 = min(nlc, a0 + 512)
                        if a1 <= a0:
                            continue
                        P.op("dve", lambda e, bi=bi, a0=a0, a1=a1: e.tensor_tensor(
                            out=T[tb][:, a0:a1], in0=PS[sb + bi][:, 0:a1 - a0], in1=bias[:, a0:a1], op=ALU.add),
                            reads=["PS%d" % (sb + bi)] + bkeys, writes=["T%d" % tb])
                    P.op("act", lambda e: e.activation(out=PT[pb][:, 0:nlc], in_=T[tb][:, 0:nlc], func=AF.Exp),
                         reads=["T%d" % tb], writes=["PT%d" % pb])
                for bi in range(nbk):
                    a0 = max(nlc, bi * 512)
                    a1 = min(nch * 128, (bi + 1) * 512)
                    if a1 <= a0:
                        continue
                    P.op("act", lambda e, bi=bi, a0=a0, a1=a1: e.activation(
                        out=PT[pb][:, a0:a1], in_=PS[sb + bi][:, a0 - bi * 512:a1 - bi * 512], func=AF.Exp),
                        reads=["PS%d" % (sb + bi)], writes=["PT%d" % pb])

            def pv(s):
                ib, h = items[s]
                chunks = binfo[ib]["chunks"]
                nch = len(chunks)
                pb = hinfo[s]
                vh = h if l == 0 else h // 4
                ob = 4 + h // 7
                oc = (h % 7) * 65
                for c, (kc_, vc_, kk, vk) in enumerate(chunks):
                    P.op("pe", lambda e, c=c, vc_=vc_: e.matmul(
                        PS[ob][:, oc:oc + 65], lhsT=PT[pb][:, c * 128:(c + 1) * 128],
                        rhs=vc_[:, vh * 65:(vh + 1) * 65], start=(c == 0), stop=(c == nch - 1)),
                        reads=["PT%d" % pb, vk], writes=["PS%d" % ob], inc=(c == nch - 1))

            groups = [(4, 0, 7), (5, 7, 7), (6, 14, 2)]

            def tailA(ib):
                for (bk, h0, nh) in groups:
                    P.op("dve", lambda e, bk=bk, h0=h0, nh=nh: e.tensor_copy(
                        out=DEN[:, h0:h0 + nh].unsqueeze(2),
                        in_=PS[bk][:, 0:nh * 65].rearrange("p (h d) -> p h d", d=65)[:, :, 64:65]),
                        reads=["PS%d" % bk], writes=["DEN"])
                if l == 1:
                    P.op("dve", lambda e: e.tensor_tensor(out=DEN, in0=DEN, in1=ESK, op=ALU.add),
                         reads=["DEN", "ESK"], writes=["DEN"])
                P.op("dve", lambda e: e.reciprocal(out=RD, in_=DEN), reads=["DEN"], writes=["RD"])
                for (bk, h0, nh) in groups:
                    P.op("dve", lambda e, bk=bk, h0=h0, nh=nh: e.tensor_tensor(
                        out=AO[:, h0 * DH:(h0 + nh) * DH].rearrange("p (h d) -> p h d", d=DH),
                        in0=PS[bk][:, 0:nh * 65].rearrange("p (h d) -> p h d", d=65)[:, :, 0:64],
                        in1=RD[:, h0:h0 + nh].unsqueeze(2).broadcast_to([128, nh, DH]), op=ALU.mult),
                        reads=["PS%d" % bk, "RD"], writes=["AO"])

            def tailB(ib):
                transposes(AO, "AO", 8)
                P.op("act", lambda e: e.copy(out=AOT, in_=PTR[:, :]), reads=["PTR"], writes=["AOT"])

            def tailC(ib):
                bi_ = binfo[ib]
                t, ty, qb = bi_["t"], bi_["ty"], bi_["qb"]
                if ty != state["ty"]:
                    load_mod(GA, (l * 2 + ty) * 6 + 2, "GA")
                    state["ty"] = ty
                for g in range(2):
                    for k in range(8):
                        P.op("pe", lambda e, g=g, k=k: e.matmul(PS[g][:, :], lhsT=AOT[:, k * 128:(k + 1) * 128],
                                                                rhs=WO[:, k, g * 512:(g + 1) * 512], start=(k == 0), stop=(k == 7)),
                             reads=["AOT", wokey(k, g * 512)], writes=["PS%d" % g], inc=(k == 7))
                for g in range(2):
                    P.op("dve", lambda e, g=g: e.tensor_tensor(out=T2[:, g * 512:(g + 1) * 512], in0=PS[g][:, :],
                                                               in1=GA[:, g * 512:(g + 1) * 512], op=ALU.mult),
                         reads=["PS%d" % g, "GA"], writes=["T2"])
                P.op("pool", lambda e: e.tensor_tensor(out=HN[qb], in0=T2, in1=HR[qb], op=ALU.add),
                     reads=["T2", "HR%d" % qb], writes=["HN%d" % qb])
                dk = ("H1_%d" if l == 0 else "H3_%d") % t
                P.dma("sync", lambda e: e.dma_start(out=dst[t][:, :], in_=HN[qb]),
                      reads=["HN%d" % qb], writes=[dk])

            ni = len(items)
            for s in range(ni + 6):
                if s < ni:
                    scores(s)
                if 0 <= s - 4 < ni:
                    pv(s - 4)
                    if items[s - 4][1] == H - 1:
                        tailA(items[s - 4][0])
                if 0 <= s - 5 < ni and items[s - 5][1] == H - 1:
                    tailB(items[s - 5][0])
                if 0 <= s - 6 < ni and items[s - 6][1] == H - 1:
                    tailC(items[s - 6][0])
            P.barrier()

        def phase3(l):
            A.reset()
            W1 = A.bf16(8 * DFF).rearrange("p (k n) -> p k n", k=8)
            W2 = A.bf16(32 * D).rearrange("p (j n) -> p j n", j=32)
            GS = A.f32(D)
            SH = A.f32(D)
            GA = A.f32(D)
            X = [A.f32(D) for _ in range(6)]
            T1 = A.f32(D)
            JK = A.bf16(D)
            SS = [A.f32(1) for _ in range(2)]
            RS = [A.f32(1) for _ in range(2)]
            U = [A.bf16(D) for _ in range(2)]
            UT2 = [A.bf16(8 * 256).rearrange("p (k n) -> p k n", k=8) for _ in range(2)]
            HT = A.bf16(32 * 256)
            R = [A.f32(512) for _ in range(2)]
            T2 = A.f32(D)
            w1key = load_w_cast(W1, w1d[l], 8, "W1_", DFF, split=4)
            w2v = w2d[l].rearrange("(j p) n -> p j n", p=128)
            for jq in range(4):
                P.dma("pool", lambda e, jq=jq: e.dma_start(out=W2[:, jq * 8:(jq + 1) * 8, :], in_=w2v[:, jq * 8:(jq + 1) * 8, :]),
                      writes=["W2_%d" % jq])
            if l == 0:
                tiles = list(range(2, 36)) + list(CT)
                src, dst = H1, H2
                sk, dk = "H1_%d", "H2_%d"
            else:
                tiles = list(range(3, 35))
                src, dst = H3, None
                sk, dk = "H3_%d", "OUT_%d"
            ngrp = len(tiles) // 2
            st = {"ty": None, "gty": None}

            def ginfo(ig):
                pair = tiles[2 * ig:2 * ig + 2]
                return pair, (1 if pair[0] in CT else 0), ig % 2

            def Ae(ig):
                pair, ty, ub = ginfo(ig)
                if ty != st["ty"]:
                    load_mod(SH, (l * 2 + ty) * 6 + 3, "SH")
                    load_mod(GS, (l * 2 + ty) * 6 + 4, "GS")
                    st["ty"] = ty
                for i, t in enumerate(pair):
                    xs = (2 * ig + i) % 6
                    P.dma("sync", lambda e, xs=xs, t=t: e.dma_start(out=X[xs], in_=src[t][:, :]),
                          reads=[sk % t], writes=["X%d" % xs])
                    norm_mod(X[xs], "X%d" % xs, JK, SS[i], RS[i], T1, U[i], GS, SH, str(i), ukey="U%d" % i)

            PTR2 = PS[6][:, :].bitcast(BF16)

            def Atr(ig):
                pair, ty, ub = ginfo(ig)
                for i, t in enumerate(pair):
                    ptr, pkey = (PTR, "PTR") if i == 0 else (PTR2, "PS6")
                    for c in range(8):
                        P.op("pe", lambda e, c=c, i=i, ptr=ptr: e.transpose(out=ptr[:, c * 128:(c + 1) * 128],
                                                                            in_=U[i][:, c * 128:(c + 1) * 128], identity=ID),
                             reads=["U%d" % i, "ID"], writes=[pkey], inc=(c == 7))
                for i, t in enumerate(pair):
                    ptr, pkey = (PTR, "PTR") if i == 0 else (PTR2, "PS6")
                    P.op("act", lambda e, i=i, ptr=ptr: e.copy(out=UT2[ub][:, :, i * 128:(i + 1) * 128],
                                                               in_=ptr[:, :].rearrange("p (k n) -> p k n", k=8)),
                         reads=[pkey], writes=["UT2_%d" % ub])

            def up(ig):
                pair, ty, ub = ginfo(ig)
                for jj in range(16):
                    hb = jj % 2
                    for j2 in range(2):
                        j = 2 * jj + j2
                        for k in range(8):
                            P.op("pe", lambda e, hb=hb, j2=j2, j=j, k=k: e.matmul(
                                PS[hb][:, j2 * 256:(j2 + 1) * 256], lhsT=W1[:, k, j * 128:(j + 1) * 128],
                                rhs=UT2[ub][:, k, :], start=(k == 0), stop=(k == 7)),
                                reads=[w1key(k, j * 128), "UT2_%d" % ub], writes=["PS%d" % hb], inc=(k == 7 and j2 == 1))
                    P.op("act", lambda e, hb=hb: e.activation(out=R[hb], in_=PS[hb][:, :], func=AF.Relu),
                         reads=["PS%d" % hb], writes=["R%d" % hb])
                    P.op("dve", lambda e, hb=hb, jj=jj: e.tensor_tensor(out=HT[:, jj * 512:(jj + 1) * 512], in0=R[hb], in1=R[hb], op=ALU.mult),
                         reads=["R%d" % hb], writes=["HT%d" % jj])

            def down(ig):
                pair, ty, ub = ginfo(ig)
                if ty != st["gty"]:
                    load_mod(GA, (l * 2 + ty) * 6 + 5, "GA")
                    st["gty"] = ty
                for i, t in enumerate(pair):
                    xs = (2 * ig + i) % 6
                    for g in range(2):
                        bk = 2 + i * 2 + g
                        for j in range(32):
                            P.op("pe", lambda e, bk=bk, j=j, i=i, g=g: e.matmul(
                                PS[bk][:, :], lhsT=HT[:, j * 256 + i * 128:j * 256 + (i + 1) * 128],
                                rhs=W2[:, j, g * 512:(g + 1) * 512], start=(j == 0), stop=(j == 31)),
                                reads=["HT%d" % (j // 2), "W2_%d" % (j // 8)], writes=["PS%d" % bk], inc=(j == 31))
                        P.op("dve", lambda e, bk=bk, g=g: e.tensor_tensor(out=T2[:, g * 512:(g + 1) * 512], in0=PS[bk][:, :],
                                                                          in1=GA[:, g * 512:(g + 1) * 512], op=ALU.mult),
                             reads=["PS%d" % bk, "GA"], writes=["T2"])
                    P.op("pool", lambda e, xs=xs: e.tensor_tensor(out=X[xs], in0=T2, in1=X[xs], op=ALU.add),
                         reads=["T2", "X%d" % xs], writes=["X%d" % xs])
                    if dst is not None:
                        P.dma("pool", lambda e, xs=xs, t=t: e.dma_start(out=dst[t][:, :], in_=X[xs]),
                              reads=["X%d" % xs], writes=[dk % t])
                    else:
                        P.dma("pool", lambda e, xs=xs, t=t: e.dma_start(out=outd[t - 3][:, :], in_=X[xs]),
                              reads=["X%d" % xs], writes=[dk % t])

            Ae(0)
            Atr(0)
            for ig in range(ngrp):
                if ig + 1 < ngrp:
                    Ae(ig + 1)
                up(ig)
                if ig + 1 < ngrp:
                    Atr(ig + 1)
                down(ig)
            P.barrier()

        plan = [(1, lambda: phase1(0)), (2, lambda: phase2(0)), (3, lambda: phase3(0)),
                (4, lambda: phase1(1)), (5, lambda: phase2(1)), (6, lambda: phase3(1))]
        for lvl, f in plan:
            if upto >= lvl:
                f()
        P.finish()

        block = es.enter_context(nc.Block())

        @block.sync
        def _(e):
            P.emit("sync", e)

        @block.scalar
        def _(e):
            P.emit("act", e)

        @block.gpsimd
        def _(e):
            P.emit("pool", e)

        @block.vector
        def _(e):
            P.emit("dve", e)

        @block.tensor
        def _(e):
            P.emit("pe", e)
    return nc


def _na_bias_tile(rpb, r_b, d):
    ka = np.arange(2)[:, None, None, None]
    kc = np.arange(64)[None, :, None, None]
    qa = np.arange(2)[None, None, :, None]
    qc = np.arange(64)[None, None, None, :]
    kr = r_b + 2 * d + ka
    qr = r_b + qa
    r0 = np.clip(qr - 4, 0, ROWS - 8)
    c0 = np.clip(qc - 8, 0, GRID_W - 16)
    valid = (kr >= r0) & (kr < r0 + 8) & (kr >= 0) & (kr < ROWS) & (qr >= 0) & (qr < ROWS) & (kc >= c0) & (kc < c0 + 16)
    ri = np.clip(kr - qr + 7, 0, 14)
    ci = np.clip(kc - qc + 15, 0, 30)
    ri, ci, valid = np.broadcast_arrays(ri, ci, valid)
    vals = rpb[:, ri, ci]
    out = np.where(valid[None], vals, np.float32(NEG)).astype(np.float32)
    return np.ascontiguousarray(out.reshape(H, 128, 128).transpose(1, 0, 2))


def _prep_core(b, half, inp):
    R0 = half * 64
    x = np.asarray(inp["x"])[b].reshape(ROWS, GRID_W, D)
    xe = np.zeros((NT, 128, D), np.float32)
    g0 = R0 - 6
    lo = max(g0, 0)
    hi = min(g0 + 76, ROWS)
    ext = np.zeros((76, GRID_W, D), np.float32)
    ext[lo - g0:hi - g0] = x[lo:hi]
    xe[:NLAT] = ext.reshape(NLAT, 128, D)
    xe[NLAT:] = np.asarray(inp["ctx"])[b].reshape(2, 128, D)
    cv = np.stack([np.asarray(inp["c"])[b], np.asarray(inp["c_ctx"])], 0)
    cT = np.ascontiguousarray(cv.reshape(2, 8, 128).transpose(2, 0, 1).reshape(128, 16))
    rpb = np.asarray(inp["na_rpb"])[0]
    bint = np.stack([_na_bias_tile(rpb, 64, d) for d in range(-2, 3)], 2)
    bint = np.ascontiguousarray(bint.reshape(128, H * 640))
    bs = []
    for t in SPECIAL:
        r_b = g0 + 2 * t
        bt = np.stack([_na_bias_tile(rpb, r_b, d) for d in range(-3, 4)], 2)
        bs.append(bt.reshape(128, H * 896))
    bspec = np.ascontiguousarray(np.stack(bs, 0))
    a = np.arange(128)[:, None]
    q = np.arange(128)[None, :]
    m_prev = np.where(q <= a, 0.0, NEG).astype(np.float32)
    m_mid = np.zeros((128, 128), np.float32)
    m_next = np.where(a <= q, 0.0, NEG).astype(np.float32)
    m_off = np.full((128, 128), NEG, np.float32)
    normal = np.concatenate([m_prev, m_mid, m_next], 1)
    first = np.concatenate([m_off if half == 0 else m_prev, m_mid, m_next], 1)
    last = np.concatenate([m_prev, m_mid, m_off if half == 1 else m_next], 1)
    mswa = np.ascontiguousarray(np.stack([normal, first, last], 0))
    tok = g0 * GRID_W + np.arange(NLAT * 128)
    row = (tok // GRID_W).astype(np.float32)
    col = (tok % GRID_W).astype(np.float32)
    inv = (np.float32(10000.0) ** (-np.arange(16, dtype=np.float32) / np.float32(16))).astype(np.float32)
    ang = np.stack([row[:, None] * inv, col[:, None] * inv], 1).astype(np.float32)
    cs = np.zeros((NT, 128, 64), np.float32)
    cs[:NLAT, :, 0:32] = np.cos(ang).reshape(NLAT, 128, 32)
    cs[:NLAT, :, 32:64] = np.sin(ang).reshape(NLAT, 128, 32)
    return {"xext": xe, "cT": cT, "bint": bint, "bspec": bspec, "mswa": mswa, "cs": cs}


def make_in_maps(inp):
    f = lambda k: np.ascontiguousarray(np.asarray(inp[k], dtype=np.float32))
    shared = {
        "ada_w": f("ada_w"), "ada_b": f("ada_b"), "g_mix": f("g_mix"), "g_mlp": f("g_mlp"),
        "mlp_w1": f("mlp_w1"), "mlp_w2": f("mlp_w2"),
        "wqkv0": f("na_wqkv")[0], "wqkv1": f("swa_wqkv")[0], "wo0": f("na_wo")[0], "wo1": f("swa_wo")[0],
        "gains": np.ascontiguousarray(np.stack([f("na_q_gain")[0], f("na_k_gain")[0], f("swa_q_gain")[0], f("swa_k_gain")[0]], 0)),
        "sink": f("swa_sink").reshape(1, H),
        "ident": np.eye(128, dtype=np.float32),
    }
    maps = []
    for core in range(8):
        b, half = core // 2, core % 2
        m = dict(shared)
        m.update(_prep_core(b, half, inp))
        maps.append(m)
    return maps


_NC_CACHE = {}


def kernel(**inputs):
    if "nc" not in _NC_CACHE:
        _NC_CACHE["nc"] = build_nc()
    nc = _NC_CACHE["nc"]
    maps = make_in_maps(inputs)
    res = run_bass_kernel_spmd(nc, maps, core_ids=list(range(8)))
    out = np.zeros((4, 8192, D), np.float32)
    for core in range(8):
        b, half = core // 2, core % 2
        out[b, half * 4096:(half + 1) * 4096] = np.asarray(res.results[core]["out"]).reshape(4096, D)
    return out
```
